# Optimizing a Trainium2 kernel written in Bass

```python
import math
import jax, jax.numpy as jnp
from jax import lax
import numpy as np

D_MODEL = 1024
BATCH = 16
SEQ = 2048
DEPTH = 4

HEAD_DIM = 64
N_HEADS_SB = D_MODEL // (2 * HEAD_DIM)
N_HEADS_FOX = D_MODEL // (2 * HEAD_DIM)
N_HEADS_DIFF = D_MODEL // (2 * HEAD_DIM)
DIFF_V_DIM = 2 * HEAD_DIM
D_FF = ((8 * D_MODEL + 3 * 256 - 1) // (3 * 256)) * 256
N_BUCKETS = 32
MAX_DISTANCE = 128
Q_BLOCK = 128
RMS_EPS = 1e-6

SB_WIDTH = N_HEADS_SB * HEAD_DIM
FOX_WIDTH = N_HEADS_FOX * HEAD_DIM
EVEN_IN = 3 * SB_WIDTH + 3 * FOX_WIDTH + N_HEADS_FOX
EVEN_MIX = SB_WIDTH + FOX_WIDTH
DIFF_QK = N_HEADS_DIFF * 2 * HEAD_DIM
DIFF_V = N_HEADS_DIFF * DIFF_V_DIM
DIFF_IN = 2 * DIFF_QK + DIFF_V
N_EVEN = (DEPTH + 1) // 2
N_ODD = DEPTH // 2

kernel_name = "hybrid_stickbreak_fox_diffattn_trunk"


def rms_norm(x, g):
    xf = x.astype(jnp.float32)
    y = xf * lax.rsqrt(jnp.mean(xf * xf, axis=-1, keepdims=True) + RMS_EPS)
    return y * g.astype(jnp.float32)


def split_heads(t, n_heads, dim):
    b, s, _ = t.shape
    return t.reshape(b, s, n_heads, dim).transpose(0, 2, 1, 3)


def merge_heads(t):
    b, h, s, d = t.shape
    return t.transpose(0, 2, 1, 3).reshape(b, s, h * d)


def sweep_query_blocks(block_fn, seq_len):
    return jnp.concatenate(
        [block_fn(i * Q_BLOCK, (i + 1) * Q_BLOCK) for i in range(seq_len // Q_BLOCK)], axis=2)


def stick_breaking_attention(q, k, v):
    scale = HEAD_DIM ** -0.5

    def block(start, end):
        z = jnp.einsum('bhqd,bhkd->bhqk', q[:, :, start:end], k[:, :, :end]) * scale
        t_pos = jnp.arange(start, end)[:, None]
        s_pos = jnp.arange(end)[None, :]
        strict = s_pos < t_pos
        log_1m_beta = jnp.where(strict, jax.nn.log_sigmoid(-z), 0.0)
        between = lax.cumsum(log_1m_beta, axis=3, reverse=True) - log_1m_beta
        w = jnp.where(strict, jnp.exp(jax.nn.log_sigmoid(z) + between), 0.0)
        return jnp.einsum('bhqk,bhkd->bhqd', w, v[:, :, :end])

    return sweep_query_blocks(block, q.shape[2])


def forgetting_attention(q, k, v, log_f):
    scale = HEAD_DIM ** -0.5
    c = jnp.cumsum(log_f, axis=-1)

    def block(start, end):
        z = jnp.einsum('bhqd,bhkd->bhqk', q[:, :, start:end], k[:, :, :end]) * scale
        z = z + c[:, :, start:end, None] - c[:, :, None, :end]
        causal = jnp.arange(end)[None, :] <= jnp.arange(start, end)[:, None]
        p = jax.nn.softmax(jnp.where(causal, z, -jnp.inf), axis=-1)
        return jnp.einsum('bhqk,bhkd->bhqd', p, v[:, :, :end])

    return sweep_query_blocks(block, q.shape[2])


def t5_bucket(dist):
    max_exact = N_BUCKETS // 2
    nf = jnp.maximum(dist, 1).astype(jnp.float32)
    large = max_exact + (jnp.log(nf / max_exact) / math.log(MAX_DISTANCE / max_exact)
                         * (N_BUCKETS - max_exact)).astype(jnp.int32)
    large = jnp.minimum(large, N_BUCKETS - 1)
    return jnp.where(dist < max_exact, dist, large)


def differential_attention(q1, q2, k1, k2, v, lam, rel_bias):
    scale = HEAD_DIM ** -0.5
    table = rel_bias.astype(jnp.float32)

    def block(start, end):
        t_pos = jnp.arange(start, end)[:, None]
        s_pos = jnp.arange(end)[None, :]
        causal = s_pos <= t_pos
        bias = table[t5_bucket(jnp.maximum(t_pos - s_pos, 0))].transpose(2, 0, 1)[None]
        s1 = jnp.einsum('bhqd,bhkd->bhqk', q1[:, :, start:end], k1[:, :, :end]) * scale + bias
        s2 = jnp.einsum('bhqd,bhkd->bhqk', q2[:, :, start:end], k2[:, :, :end]) * scale + bias
        p1 = jax.nn.softmax(jnp.where(causal, s1, -jnp.inf), axis=-1)
        p2 = jax.nn.softmax(jnp.where(causal, s2, -jnp.inf), axis=-1)
        return jnp.einsum('bhqk,bhkd->bhqd', p1 - lam * p2, v[:, :, :end])

    return sweep_query_blocks(block, q1.shape[2])


def even_mixer(h, w_in, forget_b, gq, gk, w_out):
    proj = jnp.einsum('bsd,de->bse', h, w_in)
    cuts = [SB_WIDTH, 2 * SB_WIDTH, 3 * SB_WIDTH, 3 * SB_WIDTH + FOX_WIDTH,
            3 * SB_WIDTH + 2 * FOX_WIDTH, 3 * SB_WIDTH + 3 * FOX_WIDTH]
    qa, ka, va, qb, kb, vb, fg = jnp.split(proj, cuts, axis=-1)
    f32 = jnp.float32
    o_a = stick_breaking_attention(split_heads(qa, N_HEADS_SB, HEAD_DIM).astype(f32),
                                   split_heads(ka, N_HEADS_SB, HEAD_DIM).astype(f32),
                                   split_heads(va, N_HEADS_SB, HEAD_DIM).astype(f32))
    qb = rms_norm(split_heads(qb, N_HEADS_FOX, HEAD_DIM), gq)
    kb = rms_norm(split_heads(kb, N_HEADS_FOX, HEAD_DIM), gk)
    log_f = jax.nn.log_sigmoid(fg.astype(f32) + forget_b.astype(f32)).transpose(0, 2, 1)
    o_b = forgetting_attention(qb, kb, split_heads(vb, N_HEADS_FOX, HEAD_DIM).astype(f32), log_f)
    o = jnp.concatenate([merge_heads(o_a), merge_heads(o_b)], axis=-1).astype(h.dtype)
    return jnp.einsum('bse,ed->bsd', o, w_out)


def diff_mixer(h, w_in, gq, gk, lq1, lk1, lq2, lk2, subln_g, w_out, rel_bias, layer_idx):
    b, s, _ = h.shape
    proj = jnp.einsum('bsd,de->bse', h, w_in)
    q, k, v = jnp.split(proj, [DIFF_QK, 2 * DIFF_QK], axis=-1)
    q = rms_norm(q.reshape(b, s, N_HEADS_DIFF, 2, HEAD_DIM), gq)
    k = rms_norm(k.reshape(b, s, N_HEADS_DIFF, 2, HEAD_DIM), gk)
    q1, q2 = q[..., 0, :].transpose(0, 2, 1, 3), q[..., 1, :].transpose(0, 2, 1, 3)
    k1, k2 = k[..., 0, :].transpose(0, 2, 1, 3), k[..., 1, :].transpose(0, 2, 1, 3)
    v = split_heads(v, N_HEADS_DIFF, DIFF_V_DIM).astype(jnp.float32)
    lam_init = 0.8 - 0.6 * math.exp(-0.3 * layer_idx)
    f32 = jnp.float32
    lam = (jnp.exp(jnp.sum(lq1.astype(f32) * lk1.astype(f32)))
           - jnp.exp(jnp.sum(lq2.astype(f32) * lk2.astype(f32))) + lam_init)
    o = differential_attention(q1, q2, k1, k2, v, lam, rel_bias)
    o = rms_norm(o, subln_g) * (1.0 - lam_init)
    return jnp.einsum('bse,ed->bsd', merge_heads(o).astype(h.dtype), w_out)


def swiglu(h, w_gate, w_up, w_down):
    a = jnp.einsum('bsd,df->bsf', h, w_gate)
    u = jnp.einsum('bsd,df->bsf', h, w_up)
    return jnp.einsum('bsf,fd->bsd', jax.nn.silu(a) * u, w_down)


def setup_inputs(seed: int = 0) -> dict:
    key = jax.random.key(seed)
    ks = jax.random.split(key, 24)
    f32 = jnp.float32

    def dense(k, shape, fan_in):
        return jax.random.normal(k, shape, f32) * fan_in ** -0.5

    def gain(k, shape):
        return 1.0 + 0.02 * jax.random.normal(k, shape, f32)

    return {
        "x": jax.random.normal(ks[0], (BATCH, SEQ, D_MODEL), f32),
        "attn_norm_g": gain(ks[1], (DEPTH, D_MODEL)),
        "ffn_norm_g": gain(ks[2], (DEPTH, D_MODEL)),
        "even_w_in": dense(ks[3], (N_EVEN, D_MODEL, EVEN_IN), D_MODEL),
        "fox_forget_b": 3.0 + 0.5 * jax.random.normal(ks[4], (N_EVEN, N_HEADS_FOX), f32),
        "fox_q_norm_g": gain(ks[5], (N_EVEN, HEAD_DIM)),
        "fox_k_norm_g": gain(ks[6], (N_EVEN, HEAD_DIM)),
        "even_w_out": dense(ks[7], (N_EVEN, EVEN_MIX, D_MODEL), EVEN_MIX),
        "diff_w_in": dense(ks[8], (N_ODD, D_MODEL, DIFF_IN), D_MODEL),
        "diff_q_norm_g": gain(ks[9], (N_ODD, HEAD_DIM)),
        "diff_k_norm_g": gain(ks[10], (N_ODD, HEAD_DIM)),
        "diff_lambda_q1": 0.1 * jax.random.normal(ks[11], (N_ODD, HEAD_DIM), f32),
        "diff_lambda_k1": 0.1 * jax.random.normal(ks[12], (N_ODD, HEAD_DIM), f32),
        "diff_lambda_q2": 0.1 * jax.random.normal(ks[13], (N_ODD, HEAD_DIM), f32),
        "diff_lambda_k2": 0.1 * jax.random.normal(ks[14], (N_ODD, HEAD_DIM), f32),
        "diff_subln_g": gain(ks[15], (N_ODD, DIFF_V_DIM)),
        "diff_w_out": dense(ks[16], (N_ODD, DIFF_V, D_MODEL), DIFF_V),
        "rel_bias": 0.5 * jax.random.normal(ks[17], (N_BUCKETS, N_HEADS_DIFF), f32),
        "ffn_w_gate": dense(ks[18], (DEPTH, D_MODEL, D_FF), D_MODEL),
        "ffn_w_up": dense(ks[19], (DEPTH, D_MODEL, D_FF), D_MODEL),
        "ffn_w_down": dense(ks[20], (DEPTH, D_FF, D_MODEL), D_FF),
    }


def reference(x, attn_norm_g, ffn_norm_g, even_w_in, fox_forget_b, fox_q_norm_g, fox_k_norm_g,
              even_w_out, diff_w_in, diff_q_norm_g, diff_k_norm_g, diff_lambda_q1, diff_lambda_k1,
              diff_lambda_q2, diff_lambda_k2, diff_subln_g, diff_w_out, rel_bias,
              ffn_w_gate, ffn_w_up, ffn_w_down):
    for layer in range(DEPTH):
        h = rms_norm(x, attn_norm_g[layer]).astype(x.dtype)
        if layer % 2 == 0:
            e = layer // 2
            mix = even_mixer(h, even_w_in[e], fox_forget_b[e], fox_q_norm_g[e], fox_k_norm_g[e],
                             even_w_out[e])
        else:
            o = layer // 2
            mix = diff_mixer(h, diff_w_in[o], diff_q_norm_g[o], diff_k_norm_g[o],
                             diff_lambda_q1[o], diff_lambda_k1[o], diff_lambda_q2[o],
                             diff_lambda_k2[o], diff_subln_g[o], diff_w_out[o], rel_bias, layer)
        x = x + mix.astype(x.dtype)
        h = rms_norm(x, ffn_norm_g[layer]).astype(x.dtype)
        x = x + swiglu(h, ffn_w_gate[layer], ffn_w_up[layer], ffn_w_down[layer]).astype(x.dtype)
    return x
```

```python
import math
import numpy as np
from contextlib import ExitStack
import concourse.bass as bass
import concourse.mybir as mybir
from concourse.bass_utils import run_bass_kernel_spmd

F32 = mybir.dt.float32
BF16 = mybir.dt.bfloat16
AF = mybir.ActivationFunctionType
ALU = mybir.AluOpType
AX = mybir.AxisListType

D = 1024
HD = 64
DFF = 2816
NJ = DFF // 128
NBK = 32
EPS = 1e-6
NEG = -30000.0
ENGS = ("pe", "act", "dve", "pool", "sp")


class _Op:
    __slots__ = ("idx", "eng", "emit", "deps", "dma", "marked", "ms", "epoch")

    def __init__(self, idx, eng, emit, dma, epoch):
        self.idx = idx
        self.eng = eng
        self.emit = emit
        self.dma = dma
        self.deps = ()
        self.marked = False
        self.ms = None
        self.epoch = epoch


class Sched:
    def __init__(self, nc, stack, n_dma_sems=6):
        self.nc = nc
        self.stack = stack
        self.ops = []
        self.last_w = {}
        self.readers = {}
        self.epoch = 0
        self.n_dma_sems = n_dma_sems
        self.eng_ops = {e: [] for e in ENGS}

    def new_epoch(self):
        self.epoch += 1

    def add(self, eng, emit, reads=(), writes=(), dma=False):
        op = _Op(len(self.ops), eng, emit, dma, self.epoch)
        deps = set()
        for k in reads:
            w = self.last_w.get(k)
            if w is not None:
                deps.add(w)
        for k in writes:
            w = self.last_w.get(k)
            if w is not None:
                deps.add(w)
            for r in self.readers.get(k, ()):
                deps.add(r)
        if eng == "pe" and not dma:
            deps = {d for d in deps if not (d.eng == "pe" and not d.dma)}
        op.deps = deps
        for k in reads:
            self.readers.setdefault(k, []).append(op)
        for k in writes:
            self.last_w[k] = op
            self.readers[k] = []
        self.eng_ops[eng].append(op)
        self.ops.append(op)
        return op

    def handoff(self, from_keys, to_keys):
        users = []
        for k in from_keys:
            w = self.last_w.get(k)
            if w is not None:
                users.append(w)
            users.extend(self.readers.get(k, ()))
        best = {}
        keep = []
        for u in users:
            if u.dma:
                keep.append(u)
            else:
                b = best.get(u.eng)
                if b is None or b.idx < u.idx:
                    best[u.eng] = u
        keep.extend(best.values())
        for k in to_keys:
            self.readers.setdefault(k, []).extend(keep)

    def finalize(self, final_waits=()):
        nc = self.nc
        for op in self.ops:
            for d in op.deps:
                d.marked = True
        for op in final_waits:
            op.marked = True
        sems = {}
        cnt = {}
        dma_cnt = {e: 0 for e in ENGS}

        def getsem(key):
            if key not in sems:
                sems[key] = self.stack.enter_context(nc.semaphore("s_%s_%s" % key))
                cnt[key] = 0
            return sems[key]

        for e in ENGS:
            for op in self.eng_ops[e]:
                if op.dma:
                    n = dma_cnt[e]
                    dma_cnt[e] += 1
                    key = ("d" + e, n % self.n_dma_sems)
                    s = getsem(key)
                    cnt[key] += 16
                    op.ms = (s, cnt[key])
                elif op.marked:
                    key = (e, op.epoch)
                    s = getsem(key)
                    cnt[key] += 1
                    op.ms = (s, cnt[key])
        self.n_sems = len(sems)
        self.max_cnt = max(cnt.values()) if cnt else 0

        def run_engine(e, engobj):
            waited = {}
            for op in self.eng_ops[e]:
                need = {}
                for d in op.deps:
                    s, v = d.ms
                    k = id(s)
                    if k not in need or need[k][1] < v:
                        need[k] = (s, v)
                if op.dma:
                    s, v = op.ms
                    if v > 16:
                        k = id(s)
                        if k not in need or need[k][1] < v - 16:
                            need[k] = (s, v - 16)
                for k, (s, v) in need.items():
                    if waited.get(k, 0) >= v:
                        continue
                    engobj.wait_ge(s, v)
                    waited[k] = v
                ins = op.emit(engobj)
                if op.ms is not None:
                    ins.then_inc(op.ms[0], 16 if op.dma else 1)
            if e == "sp":
                for op in final_waits:
                    engobj.wait_ge(op.ms[0], op.ms[1])

        with nc.Block() as block:
            @block.tensor
            def _(eng):
                run_engine("pe", eng)

            @block.scalar
            def _(eng):
                run_engine("act", eng)

            @block.vector
            def _(eng):
                run_engine("dve", eng)

            @block.gpsimd
            def _(eng):
                run_engine("pool", eng)

            @block.sync
            def _(eng):
                run_engine("sp", eng)


def param_layout():
    off = {}
    n = 0
    for name, w in (("g_attn", 32), ("g_ffn", 32), ("fox_gq", 2), ("fox_gk", 2), ("fox_b", 2),
                    ("diff_gq", 2), ("diff_gk", 2), ("subln", 2),
                    ("lq1", 128), ("lk1", 128), ("lq2", 128), ("lk2", 128), ("relb", 256),
                    ("m1", 1), ("m2", 1), ("selA", 8), ("selB", 8), ("bk", 256), ("mask0", 256)):
        off[name] = n
        n += w
    return off, n


def t5_bucket_np(dist):
    max_exact = NBK // 2
    nf = np.maximum(dist, 1).astype(np.float32)
    large = max_exact + (np.log(nf / max_exact) / math.log(128 / max_exact) * (NBK - max_exact)).astype(np.int32)
    large = np.minimum(large, NBK - 1)
    return np.where(dist < max_exact, dist, large)


def tile_w(W):
    K, N = W.shape
    t = W.reshape(K // 128, 128, N // 128, 128).transpose(2, 1, 0, 3)
    return np.ascontiguousarray(t).reshape(N // 128, 128, K)


def build_params(inp):
    off, n = param_layout()
    P = np.zeros((128, n), np.float32)
    p = np.arange(128)
    P[:, off["g_attn"]:off["g_attn"] + 32] = inp["attn_norm_g"].reshape(4, 8, 128).transpose(2, 0, 1).reshape(128, 32)
    P[:, off["g_ffn"]:off["g_ffn"] + 32] = inp["ffn_norm_g"].reshape(4, 8, 128).transpose(2, 0, 1).reshape(128, 32)
    P[:, off["fox_gq"]:off["fox_gq"] + 2] = inp["fox_q_norm_g"][:, p % 64].T
    P[:, off["fox_gk"]:off["fox_gk"] + 2] = inp["fox_k_norm_g"][:, p % 64].T
    fb = np.zeros((128, 2), np.float32)
    for g in range(4):
        fb[g * 32:g * 32 + 8, :] = inp["fox_forget_b"].T
    P[:, off["fox_b"]:off["fox_b"] + 2] = fb
    P[:, off["diff_gq"]:off["diff_gq"] + 2] = inp["diff_q_norm_g"][:, p % 64].T
    P[:, off["diff_gk"]:off["diff_gk"] + 2] = inp["diff_k_norm_g"][:, p % 64].T
    P[:, off["subln"]:off["subln"] + 2] = inp["diff_subln_g"].T
    for nm, key in (("lq1", "diff_lambda_q1"), ("lk1", "diff_lambda_k1"), ("lq2", "diff_lambda_q2"), ("lk2", "diff_lambda_k2")):
        P[:, off[nm]:off[nm] + 128] = np.broadcast_to(inp[key].reshape(1, 128), (128, 128))
    P[:, off["relb"]:off["relb"] + 256] = np.broadcast_to(inp["rel_bias"].reshape(1, 256), (128, 256))
    m1 = np.zeros(128, np.float32)
    m2 = np.zeros(128, np.float32)
    m1[0:8] = -1.0
    m1[64:72] = 1.0
    m2[32:40] = -1.0
    m2[96:104] = 1.0
    P[:, off["m1"]] = m1
    P[:, off["m2"]] = m2
    for h in range(8):
        a = np.zeros(128, np.float32)
        b = np.zeros(128, np.float32)
        a[h] = 1.0
        a[32 + h] = 1.0
        b[64 + h] = 1.0
        b[96 + h] = 1.0
        P[:, off["selA"] + h] = a
        P[:, off["selB"] + h] = b
    s = np.arange(128)[:, None]
    t = np.arange(128)[None, :]
    d0 = t - s
    d1 = t - s + 128
    bk = np.concatenate([t5_bucket_np(np.maximum(d0, 0)), t5_bucket_np(d1)], axis=1).astype(np.float32)
    P[:, off["bk"]:off["bk"] + 256] = bk
    mask0 = np.zeros((128, 256), np.float32)
    mask0[:, 0:128] = np.where(s > t, NEG, 0.0)
    P[:, off["mask0"]:off["mask0"] + 256] = mask0
    return P


def build_cmat():
    j = np.arange(128)[:, None]
    s = np.arange(128)[None, :]
    ident = (j == s).astype(np.float32)
    negtri = np.where(j >= s, -1.0, 0.0).astype(np.float32)
    negones = -np.ones((128, 128), np.float32)
    ones = np.ones((128, 128), np.float32)
    blk = ((j // 64) == (s // 64)).astype(np.float32)
    maskS = np.where(j >= s, NEG, 0.0).astype(np.float32)
    maskI = np.where(j > s, NEG, 0.0).astype(np.float32)
    return np.concatenate([ident, negtri, negones, ones, blk, maskS, maskI], axis=1)


C_ID, C_NTRI, C_NONES, C_ONES, C_BLK, C_MS, C_MI = range(7)


def build_program(S=2048, NSEQ=2, LAYERS=4):
    assert S % 512 == 0
    NTC = S // 512
    NKB = S // 128
    nc = bass.Bass("TRN2", target_bir_lowering=False)
    poff, NP = param_layout()

    def din(name, shape):
        return nc.dram_tensor(name, list(shape), F32, kind="ExternalInput").ap()

    xT_d = din("xT", (NSEQ, 8, 128, S))
    params_d = din("params", (128, NP))
    cmat_d = din("cmat", (128, 7 * 128))
    win_e_d = din("win_e", (2, 24, 128, 1024))
    wfg_d = din("wfg", (2, 128, 1024))
    wout_e_d = din("wout_e", (2, 1024, 1024))
    win_d_d = din("win_d", (2, 24, 128, 1024))
    wout_d_d = din("wout_d", (2, 1024, 1024))
    wg_d = din("wg", (4, NJ, 128, 1024))
    wu_d = din("wu", (4, NJ, 128, 1024))
    wd_d = din("wd", (4, 8, 128, DFF))
    yT_d = nc.dram_tensor("yT", [NSEQ, 8, 128, S], F32, kind="ExternalOutput").ap()

    st = ExitStack()
    with st:
        Sc = Sched(nc, st)
        add = Sc.add

        def sb(name, n, dt):
            return st.enter_context(nc.sbuf_tensor("sb_" + name, [128, n], dt))

        xT = sb("xT", 8 * S, F32)
        hT = sb("hT", 8 * S, BF16)
        prm = sb("prm", NP, F32)
        cm = sb("cm", 7 * 128, BF16)
        rs2 = sb("rs2", 512, F32)
        sq2 = sb("sq2", 512, BF16)
        biasT = sb("biasT", 8 * 256, BF16)
        small = sb("small", 64, F32)
        C_EPS, C_ONE, C_NEGB0, C_NEGB1, C_GQ8F0, C_GQ8F1, C_GQ8D0, C_GQ8D1 = range(8)
        C_NLAM = 8
        C_GS = 10
        C_TMP = 12
        A16N = 38912
        A32N = 4864
        a16 = sb("a16", A16N, BF16)
        a32 = sb("a32", A32N, F32)
        ps = st.enter_context(nc.psum_tensor("ps", [128, 4096], F32))

        def bank(b):
            return ps[:, b * 512:(b + 1) * 512]

        o = 0

        def carve(n):
            nonlocal o
            r = o
            o += n
            return r

        OTC = [carve(S), carve(S)]
        QT = [carve(S), carve(S)]
        KT = [carve(S), carve(S)]
        VV = [carve(NKB * 128), carve(NKB * 128)]
        QH = [carve(S), carve(S)]
        KH = [carve(S), carve(S)]
        CALL = carve(S)
        KP2 = [QH[1], KH[1]]
        WSLOT = [carve(1024) for _ in range(5)]
        LP = [carve(512), carve(512)]
        LSUM = [carve(512), carve(512)]
        WT = [carve(512) for _ in range(4)]
        SQ = [carve(512), carve(512)]
        HI = carve(512)
        LO = carve(512)
        TMPB = carve(512)
        ATT_END = o
        assert ATT_END <= A16N, ATT_END
        o = 0
        GT = carve(NJ * 1024)
        FW = [carve(1024) for _ in range(6)]
        WD = [carve(DFF), carve(DFF)]
        SQF = [carve(512), carve(512)]
        assert o <= A16N, o
        o = 0
        EB = [carve(512), carve(512)]
        RS = [carve(512), carve(512)]
        LG = carve(512)
        CP = [carve(512), carve(512)]
        T1 = carve(512)
        OD = carve(256)
        RD = carve(512)
        assert o <= A32N, o
        STM = [EB[0], EB[1]]

        def A(off, n):
            return a16[:, off:off + n]

        def A3(off, n):
            return a32[:, off:off + n]

        def cmat(i):
            return cm[:, i * 128:(i + 1) * 128]

        def pcol(name, j=0):
            c = poff[name] + j
            return prm[:, c:c + 1]

        def scol(j):
            return small[:, j:j + 1]

        ARENA_ATT = ["att16"]
        ARENA_FFN = ["ffn16"]

        def mm(out, lhsT, rhs, start, stop, reads, writes, skip=False):
            return add("pe", lambda e: e.matmul(out, lhsT=lhsT, rhs=rhs, start=start, stop=stop,
                                                skip_group_check=skip), reads, writes)

        def act(out, in_, func, reads, writes, bias=None, scale=None):
            kw = {}
            if bias is not None:
                kw["bias"] = bias
            if scale is not None:
                kw["scale"] = scale
            return add("act", lambda e: e.activation(out=out, in_=in_, func=func, **kw), reads, writes)

        def dma(eng, out, in_, reads, writes):
            return add(eng, lambda e: e.dma_start(out=out, in_=in_), reads, writes, dma=True)

        dma("sp", prm[:, :], params_d, [], ["prm"])
        dma("pool", cm[:, :], cmat_d, [], ["cm"])
        add("dve", lambda e: e.memset(small[:, C_EPS:C_EPS + 1], EPS), [], ["small"])
        add("dve", lambda e: e.memset(small[:, C_ONE:C_ONE + 1], 1.0), [], ["small"])
        for e_ in range(2):
            add("dve", lambda e, e_=e_: e.tensor_scalar(out=small[:, C_NEGB0 + e_:C_NEGB0 + e_ + 1], in0=pcol("fox_b", e_),
                                                        scalar1=-1.0, scalar2=None, op0=ALU.mult),
                ["prm", "small"], ["small"])
            add("dve", lambda e, e_=e_: e.tensor_scalar(out=small[:, C_GQ8F0 + e_:C_GQ8F0 + e_ + 1], in0=pcol("fox_gq", e_),
                                                        scalar1=0.125, scalar2=None, op0=ALU.mult),
                ["prm", "small"], ["small"])
            add("dve", lambda e, e_=e_: e.tensor_scalar(out=small[:, C_GQ8D0 + e_:C_GQ8D0 + e_ + 1], in0=pcol("diff_gq", e_),
                                                        scalar1=0.125, scalar2=None, op0=ALU.mult),
                ["prm", "small"], ["small"])

        gen_state = {"banks": [7], "i": 0}

        def gbank():
            b = gen_state["banks"][gen_state["i"] % len(gen_state["banks"])]
            gen_state["i"] += 1
            return b

        wslot_i = [0]

        def load_w(src_ap, n=1024, slots=WSLOT, ctr=wslot_i, tag="ws"):
            i = ctr[0] % len(slots)
            ctr[0] += 1
            key = (tag, i)
            dma("pool", A(slots[i], n), src_ap, [], [key])
            return slots[i], key

        def norm(gname, l, sqbufs):
            for tc in range(NTC):
                nb = 7
                for c in range(8):
                    sq = sqbufs[c % 2]
                    xs = xT[:, c * S + tc * 512: c * S + tc * 512 + 512]
                    if c % 2 == 0:
                        add("pool", lambda e, sq=sq, xs=xs: e.tensor_tensor(out=A(sq, 512), in0=xs, in1=xs, op=ALU.mult),
                            [("xT", c, tc)], [("sq", sq)])
                    else:
                        act(A(sq, 512), xs, AF.Square, [("xT", c, tc)], [("sq", sq)])
                    mm(bank(nb), cmat(C_ONES), A(sq, 512), c == 0, c == 7, [("sq", sq), "cm"], [("ps", nb)])
                act(A3(RS[0], 512), bank(nb), AF.Ln, [("ps", nb), "small"], [("rs", 0)], bias=scol(C_EPS), scale=1.0 / D)
                act(A3(RS[1], 512), A3(RS[0], 512), AF.Exp, [("rs", 0)], [("rs", 1)], scale=-0.5)
                for c in range(8):
                    xs = xT[:, c * S + tc * 512: c * S + tc * 512 + 512]
                    hs = hT[:, c * S + tc * 512: c * S + tc * 512 + 512]
                    gc = pcol(gname, l * 8 + c)
                    add("dve", lambda e, xs=xs, hs=hs, gc=gc: e.scalar_tensor_tensor(
                        out=hs, in0=xs, scalar=gc, in1=A3(RS[1], 512), op0=ALU.mult, op1=ALU.mult),
                        [("xT", c, tc), ("rs", 1), "prm"], [("hT", c, tc)])
                yield "tc"

        def adv(gen):
            for v in gen:
                if v == "tc":
                    return

        def proj_feat(wkey, woff, dst_off, mode, gcol=None, dst2_off=None):
            for tc in range(NTC):
                b = gbank()
                for kc in range(8):
                    mm(bank(b), A(woff, 1024)[:, kc * 128:(kc + 1) * 128], hT[:, kc * S + tc * 512: kc * S + tc * 512 + 512],
                       kc == 0, kc == 7, [wkey, ("hT", kc, tc)], [("ps", b)])
                yield
                if dst2_off is None:
                    parts = [(slice(0, 128), dst_off)]
                else:
                    parts = [(slice(0, 64), dst_off), (slice(64, 128), dst2_off)]
                if mode == "qknorm":
                    sq = SQ[tc % 2]
                    act(A(sq, 512), bank(b), AF.Square, [("ps", b)], [("sq", sq)])
                    b2 = gbank()
                    mm(bank(b2), cmat(C_BLK), A(sq, 512), True, True, [("sq", sq), "cm"], [("ps", b2)])
                    yield
                    act(A3(RS[0], 512), bank(b2), AF.Ln, [("ps", b2), "small"], [("rs", 0)], bias=scol(C_EPS), scale=1.0 / HD)
                    act(A3(RS[1], 512), A3(RS[0], 512), AF.Exp, [("rs", 0)], [("rs", 1)], scale=-0.5)
                    yield
                for (rs_, doff) in parts:
                    dst = A(doff, S)[rs_, tc * 512:(tc + 1) * 512]
                    src = bank(b)[rs_, :]
                    dkey = ("buf", doff, tc)
                    if mode == "scale8":
                        add("dve", lambda e, dst=dst, src=src: e.tensor_scalar(out=dst, in0=src, scalar1=0.125, scalar2=None,
                                                                               op0=ALU.mult), [("ps", b)], [dkey])
                    elif mode == "copy":
                        add("dve", lambda e, dst=dst, src=src: e.tensor_copy(out=dst, in_=src), [("ps", b)], [dkey])
                    else:
                        add("dve", lambda e, dst=dst, src=src, rs_=rs_, gcol=gcol: e.scalar_tensor_tensor(
                            out=dst, in0=src, scalar=gcol[rs_, :], in1=A3(RS[1], 512)[rs_, :], op0=ALU.mult, op1=ALU.mult),
                            [("ps", b), ("rs", 1), "prm", "small"], [dkey])
                yield "tc"

        def proj_v(wkey, woff, dst_off):
            for g in range(NKB // 4):
                b = gbank()
                for i in range(4):
                    kb = 4 * g + i
                    for kc in range(8):
                        mm(bank(b)[:, i * 128:(i + 1) * 128], hT[:, kc * S + kb * 128: kc * S + kb * 128 + 128],
                           A(woff, 1024)[:, kc * 128:(kc + 1) * 128], kc == 0, kc == 7,
                           [wkey, ("hT", kc, kb // 4)], [("ps", b)], skip=True)
                    if i % 2 == 1:
                        yield
                dst = A(dst_off, NKB * 128)[:, g * 512:(g + 1) * 512]
                add("dve", lambda e, dst=dst, b=b: e.tensor_copy(out=dst, in_=bank(b)), [("ps", b)], [("buf", dst_off, g)])
                yield "tc"

        def out_proj_partial(w_rows_ap, otc_off):
            woff, wkey = load_w(w_rows_ap)
            for m in range(8):
                for tc in range(NTC):
                    b = gbank()
                    mm(bank(b), A(woff, 1024)[:, m * 128:(m + 1) * 128], A(otc_off, S)[:, tc * 512:(tc + 1) * 512],
                       True, True, [wkey, ("buf", otc_off, tc)], [("ps", b)])
                    xs = xT[:, m * S + tc * 512: m * S + tc * 512 + 512]
                    add("dve", lambda e, xs=xs, b=b: e.tensor_tensor(out=xs, in0=xs, in1=bank(b), op=ALU.add),
                        [("ps", b), ("xT", m, tc)], [("xT", m, tc)])
                    if tc % 2 == 1:
                        yield

        bgq = []

        def pump(n=1):
            for _ in range(n):
                while bgq:
                    try:
                        next(bgq[0])
                        break
                    except StopIteration:
                        bgq.pop(0)

        def drain():
            while bgq:
                pump()

        def run(gen):
            for _ in gen:
                pass

        def vkeys(voff):
            return [("buf", voff, g) for g in range(NKB // 4)]

        def tkeys(off):
            return [("buf", off, tc) for tc in range(NTC)]

        ones32 = A3(LG, 512)[:, 0:128]

        def wsum_bufs(qc):
            if qc % 2 == 0:
                return (EB[0], ("eb", 0)), (EB[1], ("eb", 1))
            return (CP[0], ("cp", 0)), (CP[1], ("cp", 1))

        def den_acc(n, qc, q0, w, wkey, two_maps=False):
            boff, bkey = wsum_bufs(qc)[n % 2]
            eng = "dve"
            full = A3(boff, 512)
            if two_maps:
                v = full.rearrange("p (m q) -> p m q", m=2)
                dstv = v[:, :, q0:256]
                zerov = v[:, :, 0:q0] if q0 > 0 else None
            else:
                dstv = full[:, q0:512]
                zerov = full[:, 0:q0] if q0 > 0 else None
            if n < 2:
                if zerov is not None:
                    add(eng, lambda e: e.memset(zerov, 0.0), [], [bkey])
                add(eng, lambda e: e.tensor_copy(out=dstv, in_=w), [wkey], [bkey])
            else:
                add(eng, lambda e: e.tensor_tensor(out=dstv, in0=dstv, in1=w, op=ALU.add), [wkey, bkey], [bkey])

        def sb_head(qoff, koff, voff, otc_off, pofs, ctr):
            pairs = []
            for qc in range(NTC):
                kbs = list(range(min(4 * qc + 3, NKB - 1), -1, -1))
                for n, kb in enumerate(kbs):
                    pairs.append(dict(qc=qc, kb=kb, q0=max(0, 128 * (kb - 4 * qc)), first=(n == 0), last=(kb == 0), n=n))
            NZ = 3
            qk_r = tkeys(qoff) + tkeys(koff)

            def zb(i):
                return i % NZ

            def stA(i):
                p = pairs[i]
                q0 = p["q0"]
                z = bank(zb(i))
                diag = p["kb"] >= 4 * p["qc"]
                mm(z[:, q0:512], A(koff, S)[:, p["kb"] * 128:(p["kb"] + 1) * 128],
                   A(qoff, S)[:, p["qc"] * 512 + q0:(p["qc"] + 1) * 512], True, not diag,
                   qk_r, [("ps", zb(i))], skip=True)
                if diag:
                    mm(z[:, q0:q0 + 128], cmat(C_ID), cmat(C_MS), False, True, ["cm"], [("ps", zb(i))], skip=True)

            def stB(i):
                p = pairs[i]
                q0 = p["q0"]
                act(A3(EB[i % 2], 512)[:, q0:512], bank(zb(i))[:, q0:512], AF.Exp, [("ps", zb(i))], [("eb", i % 2)])

            def stC(i):
                p = pairs[i]
                q0 = p["q0"]
                act(A(LP[i % 2], 512)[:, q0:512], A3(EB[i % 2], 512)[:, q0:512], AF.Ln, [("eb", i % 2), "small"],
                    [("lp", i % 2)], bias=scol(C_ONE))

            def stD(i):
                p = pairs[i]
                q0 = p["q0"]
                z = bank(zb(i))
                if p["first"]:
                    for k in range(2):
                        add("pool", lambda e, k=k: e.memset(A(LSUM[k], 512), 0.0), [], [("lsum", k)])
                cur = p["n"] % 2
                prv = 1 - cur
                mm(z[:, q0:512], cmat(C_NTRI), A(LP[i % 2], 512)[:, q0:512], False, p["first"],
                   [("lp", i % 2), "cm", ("ps", zb(i))], [("ps", zb(i))], skip=True)
                if not p["first"]:
                    mm(z[:, q0:512], cmat(C_NONES), A(LSUM[prv], 512)[:, q0:512], False, True,
                       [("lsum", prv), "cm", ("ps", zb(i))], [("ps", zb(i))], skip=True)
                if not p["last"]:
                    lpi = LP[i % 2]
                    if p["first"]:
                        add("pool", lambda e, lpi=lpi, cur=cur, q0=q0: e.tensor_copy(out=A(LSUM[cur], 512)[:, q0:512],
                                                                                     in_=A(lpi, 512)[:, q0:512]),
                            [("lp", i % 2)], [("lsum", cur)])
                    else:
                        add("pool", lambda e, lpi=lpi, cur=cur, prv=prv, q0=q0: e.tensor_tensor(
                            out=A(LSUM[cur], 512)[:, q0:512], in0=A(LSUM[prv], 512)[:, q0:512], in1=A(lpi, 512)[:, q0:512],
                            op=ALU.add), [("lp", i % 2), ("lsum", prv)], [("lsum", cur)])

            def stE(i):
                p = pairs[i]
                q0 = p["q0"]
                act(A(WT[i % 4], 512)[:, q0:512], bank(zb(i))[:, q0:512], AF.Exp, [("ps", zb(i))], [("wt", i % 4)])

            def stF(i):
                p = pairs[i]
                q0 = p["q0"]
                if p["first"]:
                    ctr[0] += 1
                ob = 3 + ctr[0] % 2
                mm(bank(ob)[pofs:pofs + 64, q0:512], A(voff, NKB * 128)[:, p["kb"] * 128 + pofs: p["kb"] * 128 + pofs + 64],
                   A(WT[i % 4], 512)[:, q0:512], p["first"], p["last"], [("wt", i % 4)] + vkeys(voff), [("ps", ob)], skip=True)
                if p["last"]:
                    dst = A(otc_off, S)[pofs:pofs + 64, p["qc"] * 512:(p["qc"] + 1) * 512]
                    add("dve", lambda e, dst=dst, ob=ob: e.tensor_copy(out=dst, in_=bank(ob)[pofs:pofs + 64, :]),
                        [("ps", ob)], [("buf", otc_off, p["qc"])])

            n = len(pairs)
            for t in range(-1, n + 2):
                if 0 <= t < n:
                    stB(t)
                if 0 <= t - 2 < n:
                    stE(t - 2)
                if 0 <= t + 1 < n:
                    stA(t + 1)
                if 0 <= t < n:
                    stC(t)
                    stD(t)
                if 0 <= t - 2 < n:
                    stF(t - 2)
                pump()

        def fox_head(qhoff, khoff, qoff, koff, voff, otc_off, pofs, ctr):
            pairs = []
            for qc in range(NTC):
                kbs = list(range(min(4 * qc + 3, NKB - 1), -1, -1))
                for n, kb in enumerate(kbs):
                    pairs.append(dict(qc=qc, kb=kb, q0=max(0, 128 * (kb - 4 * qc)), first=(n == 0), last=(kb == 0), n=n))
            ZB = [0, 1, 6]
            qk_r = tkeys(qoff) + tkeys(koff)

            def stA(i):
                p = pairs[i]
                q0 = p["q0"]
                z = bank(ZB[i % 3])
                diag = p["kb"] >= 4 * p["qc"]
                ks = slice(p["kb"] * 128, (p["kb"] + 1) * 128)
                qs = slice(p["qc"] * 512 + q0, (p["qc"] + 1) * 512)
                mm(z[:, q0:512], A(koff, S)[:, ks], A(qoff, S)[:, qs], True, False,
                   qk_r, [("ps", ZB[i % 3])], skip=True)
                mm(z[:, q0:512], A(khoff, S)[:, ks], A(qhoff, S)[:, qs], False, not diag,
                   [("aug", khoff), ("aug", qhoff)], [("ps", ZB[i % 3])], skip=True)
                if diag:
                    mm(z[:, q0:q0 + 128], cmat(C_ID), cmat(C_MI), False, True, ["cm"], [("ps", ZB[i % 3])], skip=True)

            def stB(i):
                p = pairs[i]
                q0 = p["q0"]
                act(A(WT[i % 4], 512)[:, q0:512], bank(ZB[i % 3])[:, q0:512], AF.Exp, [("ps", ZB[i % 3])], [("wt", i % 4)])

            def stF(i):
                p = pairs[i]
                q0 = p["q0"]
                if p["first"]:
                    ctr[0] += 1
                ob = 3 + ctr[0] % 2
                db = 5
                w = A(WT[i % 4], 512)[:, q0:512]
                mm(bank(ob)[pofs:pofs + 64, q0:512], A(voff, NKB * 128)[:, p["kb"] * 128 + pofs: p["kb"] * 128 + pofs + 64],
                   w, p["first"], p["last"], [("wt", i % 4)] + vkeys(voff), [("ps", ob)], skip=True)
                den_acc(p["n"], p["qc"], q0, w, ("wt", i % 4))
                if p["last"]:
                    (b0, k0), (b1, k1) = wsum_bufs(p["qc"])
                    mm(bank(db), ones32, A3(b0, 512), True, False, [k0, "lg"], [("ps", db)], skip=True)
                    mm(bank(db), ones32, A3(b1, 512), False, True, [k1, "lg"], [("ps", db)], skip=True)
                    rd = A3(RD, 512)[pofs:pofs + 64, :]
                    act(rd, bank(db)[pofs:pofs + 64, :], AF.Ln, [("ps", db)], ["rd"])
                    act(rd, rd, AF.Exp, ["rd"], ["rd"], scale=-1.0)
                    dst = A(otc_off, S)[pofs:pofs + 64, p["qc"] * 512:(p["qc"] + 1) * 512]
                    add("dve", lambda e, dst=dst, ob=ob, rd=rd: e.tensor_tensor(out=dst, in0=bank(ob)[pofs:pofs + 64, :],
                                                                               in1=rd, op=ALU.mult),
                        [("ps", ob), "rd"], [("buf", otc_off, p["qc"])])

            n = len(pairs)
            for t in range(-2, n):
                if 0 <= t + 2 < n:
                    stA(t + 2)
                if 0 <= t + 1 < n:
                    stB(t + 1)
                if 0 <= t < n:
                    stF(t)
                pump()

        def diff_head(h, qoff, koff, koff2, voff, otc_off, o_idx, ctr):
            NQ = S // 256
            pairs = []
            for qc in range(NQ):
                kbs = list(range(min(2 * qc + 1, NKB - 1), -1, -1))
                for n, kb in enumerate(kbs):
                    pairs.append(dict(qc=qc, kb=kb, q0=(128 if kb == 2 * qc + 1 else 0), first=(n == 0), last=(kb == 0), n=n))
            ZB = [0, 1, 6]
            qk_r = tkeys(qoff) + tkeys(koff) + tkeys(koff2)
            bdiag = biasT[:, h * 256: h * 256 + 128]
            boff = biasT[:, h * 256 + 128: h * 256 + 256]

            def v2(ap, q0):
                return ap.rearrange("p (m q) -> p m q", m=2)[:, :, q0:256]

            def stA(i):
                p = pairs[i]
                q0 = p["q0"]
                z = bank(ZB[i % 3])
                ks = slice(p["kb"] * 128, (p["kb"] + 1) * 128)
                qs = slice(p["qc"] * 256 + q0, (p["qc"] + 1) * 256)
                d = p["kb"] - 2 * p["qc"]
                near = d >= -1
                mm(z[:, q0:256], A(koff, S)[:, ks], A(qoff, S)[:, qs], True, False, qk_r, [("ps", ZB[i % 3])], skip=True)
                mm(z[:, 256 + q0:512], A(koff2, S)[:, ks], A(qoff, S)[:, qs], False, not near, qk_r,
                   [("ps", ZB[i % 3])], skip=True)
                if near:
                    blocks = []
                    if d == 1:
                        blocks = [(128, bdiag)]
                    elif d == 0:
                        blocks = [(0, bdiag), (128, boff)]
                    else:
                        blocks = [(0, boff)]
                    k = 0
                    for m_ in range(2):
                        for (c0, bt) in blocks:
                            k += 1
                            mm(z[:, m_ * 256 + c0: m_ * 256 + c0 + 128], cmat(C_ID), bt, False, k == 2 * len(blocks),
                               ["cm", "biasT"], [("ps", ZB[i % 3])], skip=True)

            def stB(i):
                p = pairs[i]
                q0 = p["q0"]
                act(v2(A(WT[i % 4], 512), q0), v2(bank(ZB[i % 3]), q0), AF.Exp, [("ps", ZB[i % 3])], [("wt", i % 4)])

            def stF(i):
                p = pairs[i]
                q0 = p["q0"]
                qc = p["qc"]
                if p["first"]:
                    ctr[0] += 1
                ob = 3 + ctr[0] % 2
                db = 5
                vk = A(voff, NKB * 128)[:, p["kb"] * 128:(p["kb"] + 1) * 128]
                if q0 == 0:
                    segs = [(0, 512)]
                else:
                    segs = [(q0, 256), (256 + q0, 512)]
                for si, (c0, c1) in enumerate(segs):
                    w = A(WT[i % 4], 512)[:, c0:c1]
                    mm(bank(ob)[:, c0:c1], vk, w, p["first"] and si == 0, p["last"] and si == len(segs) - 1,
                       [("wt", i % 4)] + vkeys(voff), [("ps", ob)], skip=True)
                den_acc(p["n"], qc, q0, v2(A(WT[i % 4], 512), q0), ("wt", i % 4), two_maps=True)
                if p["last"]:
                    exhaust_tails()
                    tailq.append(chunk_tail(ob, db, qc))

            tailq = []

            def pump_tail():
                if tailq:
                    try:
                        next(tailq[0])
                    except StopIteration:
                        tailq.pop(0)

            def exhaust_tails():
                while tailq:
                    pump_tail()

            def chunk_tail(ob, db, qc):
                (b0, k0), (b1, k1) = wsum_bufs(qc)
                mm(bank(db), ones32, A3(b0, 512), True, False, [k0, "lg"], [("ps", db)], skip=True)
                mm(bank(db), ones32, A3(b1, 512), False, True, [k1, "lg"], [("ps", db)], skip=True)
                yield
                act(A3(RD, 512), bank(db), AF.Ln, [("ps", db)], ["rd"])
                act(A3(RD, 512), A3(RD, 512), AF.Exp, ["rd"], ["rd"], scale=-1.0)
                yield
                add("dve", lambda e, ob=ob: e.tensor_tensor(out=A3(T1, 512), in0=bank(ob), in1=A3(RD, 512), op=ALU.mult),
                    [("ps", ob), "rd"], ["t1"])
                add("dve", lambda e: e.scalar_tensor_tensor(out=A3(OD, 256), in0=A3(T1, 512)[:, 256:512],
                                                            scalar=scol(C_NLAM + o_idx), in1=A3(T1, 512)[:, 0:256],
                                                            op0=ALU.mult, op1=ALU.add), ["t1", "small"], ["od"])
                yield
                sqa = sq2[:, (qc % 2) * 256:(qc % 2) * 256 + 256]
                sqk = ("sq2", qc % 2)
                add("pool", lambda e, sqa=sqa: e.tensor_tensor(out=sqa, in0=A3(OD, 256), in1=A3(OD, 256),
                                                               op=ALU.mult), ["od"], [sqk])
                yield
                nb = db
                mm(bank(nb)[:, 0:256], cmat(C_ONES), sqa, True, True, [sqk, "cm"], [("ps", nb)])
                yield
                act(rs2[:, 0:256], bank(nb)[:, 0:256], AF.Ln, [("ps", nb), "small"], [("rs2", 0)],
                    bias=scol(C_EPS), scale=1.0 / 128)
                act(rs2[:, 256:512], rs2[:, 0:256], AF.Exp, [("rs2", 0)], [("rs2", 1)], scale=-0.5)
                yield
                dst = A(otc_off, S)[:, qc * 256:(qc + 1) * 256]
                add("dve", lambda e, dst=dst: e.scalar_tensor_tensor(out=dst, in0=A3(OD, 256), scalar=scol(C_GS + o_idx),
                                                                     in1=rs2[:, 256:512], op0=ALU.mult, op1=ALU.mult),
                    ["od", ("rs2", 1), "small"], [("buf", otc_off, qc // 2)])

            n = len(pairs)
            for t in range(-2, n):
                if 0 <= t + 2 < n:
                    stA(t + 2)
                if 0 <= t + 1 < n:
                    stB(t + 1)
                if 0 <= t < n:
                    stF(t)
                pump_tail()
                pump()
            exhaust_tails()

        def even_layer(e_, ng):
            octr = [0]
            woff, wkey = load_w(wfg_d[e_])
            gen_state["banks"] = [5, 6, 7]

            def foxc(tc):
                b = gbank()
                for kc in range(8):
                    mm(bank(b), A(woff, 1024)[:, kc * 128:(kc + 1) * 128], hT[:, kc * S + tc * 512: kc * S + tc * 512 + 512],
                       kc == 0, kc == 7, [wkey, ("hT", kc, tc)], [("ps", b)])
                act(A3(EB[0], 512), bank(b), AF.Exp, [("ps", b), "small"], [("eb", 0)], bias=scol(C_NEGB0 + e_), scale=-1.0)
                act(A3(LG, 512), A3(EB[0], 512), AF.Ln, [("eb", 0), "small"], ["lg"], bias=scol(C_ONE))
                cpb = CP[tc % 2]
                init = 0.0 if tc == 0 else A3(CP[(tc - 1) % 2], 512)[:, 511:512]
                add("dve", lambda e, cpb=cpb, init=init: e.tensor_tensor_scan(
                    out=A3(cpb, 512), data0=small[:, C_ONE:C_ONE + 1].to_broadcast([128, 512]), data1=A3(LG, 512),
                    initial=init, op0=ALU.mult, op1=ALU.add),
                    ["lg", "small", ("cp", (tc - 1) % 2)], [("cp", tc % 2)])
                add("dve", lambda e, cpb=cpb: e.tensor_copy(out=A(HI, 512), in_=A3(cpb, 512)), [("cp", tc % 2)], ["hi"])
                add("dve", lambda e, cpb=cpb: e.tensor_tensor(out=A(LO, 512), in0=A3(cpb, 512), in1=A(HI, 512), op=ALU.subtract),
                    [("cp", tc % 2), "hi"], ["lo"])
                add("dve", lambda e: e.tensor_scalar(out=A(TMPB, 512), in0=A(HI, 512), scalar1=pcol("m1"), scalar2=None,
                                                     op0=ALU.mult), ["hi", "prm"], ["tmpb"])
                add("dve", lambda e, tc=tc: e.scalar_tensor_tensor(out=A(CALL, S)[:, tc * 512:(tc + 1) * 512], in0=A(LO, 512),
                                                                  scalar=pcol("m2"), in1=A(TMPB, 512), op0=ALU.mult, op1=ALU.add),
                    ["lo", "tmpb", "prm"], [("call", tc)])

            def pair_inproj(pi):
                fox = pi >= 4
                c = pi % 4
                buf = pi % 2
                jq, jk, jv = (12 + c, 16 + c, 20 + c) if fox else (c, 4 + c, 8 + c)
                wq, kq = load_w(win_e_d[e_, jq])
                wk, kk = load_w(win_e_d[e_, jk])
                wv, kv = load_w(win_e_d[e_, jv])
                if fox:
                    yield from proj_feat(kq, wq, QT[buf], "qknorm", scol(C_GQ8F0 + e_))
                    yield from proj_feat(kk, wk, KT[buf], "qknorm", pcol("fox_gk", e_), dst2_off=KP2[buf])
                else:
                    yield from proj_feat(kq, wq, QT[buf], "scale8")
                    yield from proj_feat(kk, wk, KT[buf], "copy", dst2_off=KP2[buf])
                yield from proj_v(kv, wv, VV[buf])

            gen_state["banks"] = [5, 6, 7]
            wq0, kq0 = load_w(win_e_d[e_, 0])
            wk0, kk0 = load_w(win_e_d[e_, 4])
            wv0, kv0 = load_w(win_e_d[e_, 8])
            g0 = [proj_feat(kq0, wq0, QT[0], "scale8"), proj_feat(kk0, wk0, KT[0], "copy", dst2_off=KP2[0]),
                  proj_v(kv0, wv0, VV[0])]
            for tc in range(NTC):
                adv(ng)
                foxc(tc)
                for g_ in g0:
                    adv(g_)
            for g_ in g0:
                run(g_)
            run(ng)
            add("dve", lambda e: e.memset(ones32, 1.0), [], ["lg"])
            for pi in range(8):
                fox = pi >= 4
                c = pi % 4
                buf = pi % 2
                gen_state["banks"] = [7, 2] if fox else [5, 6, 7]
                if pi >= 1:
                    bgq.append(out_proj_partial(wout_e_d[e_, (pi - 1) * 128:pi * 128, :], OTC[1 - buf]))
                if pi + 1 < 8:
                    bgq.append(pair_inproj(pi + 1))
                for hh in range(2):
                    pofs = 64 * hh
                    if fox:
                        h = 2 * c + hh
                        calls = [("call", tc) for tc in range(NTC)]
                        add("dve", lambda e, h=h: e.tensor_scalar(out=A(QH[0], S), in0=A(CALL, S), scalar1=pcol("selA", h),
                                                                  scalar2=pcol("selB", h), op0=ALU.mult, op1=ALU.add),
                            calls + ["prm"], [("aug", QH[0])])
                        add("dve", lambda e, h=h: e.tensor_scalar(out=A(KH[0], S), in0=A(CALL, S), scalar1=pcol("selB", h),
                                                                  scalar2=pcol("selA", h), op0=ALU.mult, op1=ALU.add),
                            calls + ["prm"], [("aug", KH[0])])
                        fox_head(QH[0], KH[0], QT[buf], (KT[buf], KP2[buf])[hh], VV[buf], OTC[buf], pofs, octr)
                    else:
                        sb_head(QT[buf], (KT[buf], KP2[buf])[hh], VV[buf], OTC[buf], pofs, octr)
                drain()
            run(out_proj_partial(wout_e_d[e_, 7 * 128:8 * 128, :], OTC[1]))

        def odd_layer(o_, l, first_time, ng):
            octr = [0]
            lam_init = 0.8 - 0.6 * math.exp(-0.3 * l)
            if first_time[0]:
                first_time[0] = False
                tb = A3(T1, 512)[:, 0:256]
                add("dve", lambda e: e.tensor_tensor(
                    out=tb.rearrange("p (b h) -> p b h", h=8),
                    in0=prm[:, poff["relb"]:poff["relb"] + 256].rearrange("p (b h) -> p b h", h=8),
                    in1=prm[:, poff["relb"] + 248:poff["relb"] + 256].unsqueeze(1).to_broadcast([128, 32, 8]),
                    op=ALU.subtract), ["prm"], ["t1"])
                acc = A3(RD, 512)[:, 0:256]
                tmp = A3(RD, 512)[:, 256:512]
                bkt = prm[:, poff["bk"]:poff["bk"] + 256]
                for h in range(8):
                    add("dve", lambda e: e.tensor_copy(out=acc, in_=prm[:, poff["mask0"]:poff["mask0"] + 256]), ["prm"], ["rd"])
                    for b_ in range(NBK):
                        col = A3(T1, 512)[:, b_ * 8 + h: b_ * 8 + h + 1]
                        add("dve", lambda e, col=col, b_=b_: e.tensor_scalar(out=tmp, in0=bkt, scalar1=float(b_), scalar2=col,
                                                                             op0=ALU.is_equal, op1=ALU.mult),
                            ["prm", "t1"], ["rdt"])
                        add("dve", lambda e: e.tensor_tensor(out=acc, in0=acc, in1=tmp, op=ALU.add), ["rd", "rdt"], ["rd"])
                    add("dve", lambda e, h=h: e.tensor_copy(out=biasT[:, h * 256:(h + 1) * 256], in_=acc), ["rd"], ["biasT"])
            for i_, (a_, b_) in enumerate((("lq1", "lk1"), ("lq2", "lk2"))):
                pa = prm[:, poff[a_] + o_ * 64: poff[a_] + o_ * 64 + 64]
                pb = prm[:, poff[b_] + o_ * 64: poff[b_] + o_ * 64 + 64]
                add("dve", lambda e, pa=pa, pb=pb: e.tensor_tensor(out=A3(OD, 256)[:, 0:64], in0=pa, in1=pb, op=ALU.mult),
                    ["prm"], ["od"])
                add("dve", lambda e, i_=i_: e.reduce_sum(out=small[:, C_TMP + i_:C_TMP + i_ + 1], in_=A3(OD, 256)[:, 0:64], axis=AX.X),
                    ["od"], ["small"])
                act(small[:, C_TMP + 2 + i_:C_TMP + 3 + i_], small[:, C_TMP + i_:C_TMP + i_ + 1], AF.Exp, ["small"], ["small"])
            add("dve", lambda e: e.tensor_tensor(out=small[:, C_TMP + 4:C_TMP + 5], in0=small[:, C_TMP + 3:C_TMP + 4],
                                                 in1=small[:, C_TMP + 2:C_TMP + 3], op=ALU.subtract), ["small"], ["small"])
            add("dve", lambda e: e.tensor_scalar(out=small[:, C_NLAM + o_:C_NLAM + o_ + 1], in0=small[:, C_TMP + 4:C_TMP + 5],
                                                 scalar1=-lam_init, scalar2=None, op0=ALU.add), ["small"], ["small"])
            add("dve", lambda e: e.tensor_scalar(out=small[:, C_GS + o_:C_GS + o_ + 1], in0=pcol("subln", o_),
                                                 scalar1=1.0 - lam_init, scalar2=None, op0=ALU.mult), ["small", "prm"], ["small"])
            gen_state["banks"] = [7, 2]
            add("dve", lambda e: e.memset(ones32, 1.0), [], ["lg"])

            def head_inproj(h):
                buf = h % 2
                wq, kq = load_w(win_d_d[o_, h])
                wk, kk = load_w(win_d_d[o_, 8 + h])
                wv, kv = load_w(win_d_d[o_, 16 + h])
                yield from proj_feat(kq, wq, QT[buf], "qknorm", scol(C_GQ8D0 + o_))
                yield from proj_feat(kk, wk, KT[buf], "qknorm", pcol("diff_gk", o_), dst2_off=KP2[buf])
                yield from proj_v(kv, wv, VV[buf])

            wq0, kq0 = load_w(win_d_d[o_, 0])
            wk0, kk0 = load_w(win_d_d[o_, 8])
            wv0, kv0 = load_w(win_d_d[o_, 16])
            g0 = [proj_feat(kq0, wq0, QT[0], "qknorm", scol(C_GQ8D0 + o_)),
                  proj_feat(kk0, wk0, KT[0], "qknorm", pcol("diff_gk", o_), dst2_off=KP2[0]),
                  proj_v(kv0, wv0, VV[0])]
            for tc in range(NTC):
                adv(ng)
                for g_ in g0:
                    adv(g_)
            for g_ in g0:
                run(g_)
            run(ng)
            for h in range(8):
                buf = h % 2
                if h >= 1:
                    bgq.append(out_proj_partial(wout_d_d[o_, (h - 1) * 128:h * 128, :], OTC[1 - buf]))
                if h + 1 < 8:
                    bgq.append(head_inproj(h + 1))
                diff_head(h, QT[buf], KT[buf], KP2[buf], VV[buf], OTC[buf], o_, octr)
                drain()
            run(out_proj_partial(wout_d_d[o_, 7 * 128:8 * 128, :], OTC[1]))

        fw_i = [0]
        wd_i = [0]

        def ffn(l, ng):
            NH = S // 1024 if S >= 1024 else 1
            TW = S // NH
            NT = TW // 512
            it = 0
            run(ng)
            for half in range(NH):
                for j in range(NJ):
                    wgo, kg = load_w(wg_d[l, j], slots=FW, ctr=fw_i, tag="fw")
                    wuo, ku = load_w(wu_d[l, j], slots=FW, ctr=fw_i, tag="fw")
                    for tl in range(NT):
                        tc = half * NT + tl
                        ba, bu = (0, 1) if it % 2 == 0 else (2, 3)
                        it += 1
                        for kc in range(8):
                            mm(bank(ba), A(wgo, 1024)[:, kc * 128:(kc + 1) * 128], hT[:, kc * S + tc * 512: kc * S + tc * 512 + 512],
                               kc == 0, kc == 7, [kg, ("hT", kc, tc)], [("ps", ba)])
                        for kc in range(8):
                            mm(bank(bu), A(wuo, 1024)[:, kc * 128:(kc + 1) * 128], hT[:, kc * S + tc * 512: kc * S + tc * 512 + 512],
                               kc == 0, kc == 7, [ku, ("hT", kc, tc)], [("ps", bu)])
                        stm = STM[it % 2]
                        act(A3(stm, 512), bank(ba), AF.Silu, [("ps", ba)], [("stm", stm)])
                        gdst = A(GT, NJ * 1024)[:, j * 1024 + tl * 512: j * 1024 + tl * 512 + 512]
                        add("dve", lambda e, gdst=gdst, stm=stm, bu=bu: e.tensor_tensor(out=gdst, in0=A3(stm, 512), in1=bank(bu),
                                                                                        op=ALU.mult),
                            [("stm", stm), ("ps", bu)], [("gT", j, tl)])
                for m in range(8):
                    wdo, kd = load_w(wd_d[l, m], n=DFF, slots=WD, ctr=wd_i, tag="wd")
                    for tl in range(NT):
                        tc = half * NT + tl
                        b = 4 + (m * NT + tl) % 2
                        for j in range(NJ):
                            mm(bank(b), A(wdo, DFF)[:, j * 128:(j + 1) * 128], A(GT, NJ * 1024)[:, j * 1024 + tl * 512: j * 1024 + tl * 512 + 512],
                               j == 0, j == NJ - 1, [kd, ("gT", j, tl)], [("ps", b)])
                        xs = xT[:, m * S + tc * 512: m * S + tc * 512 + 512]
                        add("dve", lambda e, xs=xs, b=b: e.tensor_tensor(out=xs, in0=xs, in1=bank(b), op=ALU.add),
                            [("ps", b), ("xT", m, tc)], [("xT", m, tc)])

        def att_keys():
            ks = []
            for off in OTC + QT + KT:
                ks += tkeys(off)
            for off in VV:
                ks += vkeys(off)
            for off in QH + KH:
                ks.append(("aug", off))
            for off in KP2:
                ks += tkeys(off)
            ks += [("call", tc) for tc in range(NTC)]
            ks += [("ws", i) for i in range(len(WSLOT))]
            ks += [("lp", 0), ("lp", 1), ("lsum", 0), ("lsum", 1), ("wt", 0), ("wt", 1), ("wt", 2), ("wt", 3)]
            ks += [("sq", SQ[0]), ("sq", SQ[1]), "hi", "lo", "tmpb"]
            return ks

        def ffn_keys():
            ks = [("gT", j, tl) for j in range(NJ) for tl in range(2)]
            ks += [("fw", i) for i in range(len(FW))] + [("wd", 0), ("wd", 1)]
            ks += [("sq", SQF[0]), ("sq", SQF[1])]
            return ks

        first_time = [True]
        outs = []
        for seq in range(NSEQ):
            for c in range(8):
                dma("sp", xT[:, c * S:(c + 1) * S], xT_d[seq, c], [], [("xT", c, tc) for tc in range(NTC)])
            for l in range(LAYERS):
                Sc.new_epoch()
                Sc.handoff(ffn_keys() + [("stm", STM[0]), ("stm", STM[1])], att_keys() + [("eb", 0), ("eb", 1)])
                for buf_ in range(2):
                    add("pool", lambda e, buf_=buf_: e.memset(A(KT[buf_], S)[64:128, :], 0.0), [], tkeys(KT[buf_]))
                    add("pool", lambda e, buf_=buf_: e.memset(A(KP2[buf_], S)[0:64, :], 0.0), [], tkeys(KP2[buf_]))
                ng = norm("g_attn", l, SQ)
                if l % 2 == 0:
                    even_layer(l // 2, ng)
                else:
                    odd_layer(l // 2, l, first_time, ng)
                Sc.handoff(att_keys() + [("eb", 0), ("eb", 1)], ffn_keys() + [("stm", STM[0]), ("stm", STM[1])])
                ffn(l, norm("g_ffn", l, SQF))
            for c in range(8):
                outs.append(dma("sp", yT_d[seq, c], xT[:, c * S:(c + 1) * S], [("xT", c, tc) for tc in range(NTC)], []))
        Sc.finalize(final_waits=outs)
        build_program.stats = dict(n_ops=len(Sc.ops), n_sems=Sc.n_sems, max_cnt=Sc.max_cnt,
                                   per_eng={e: len(v) for e, v in Sc.eng_ops.items()})
    return nc


def make_shared_inputs(inp):
    f = lambda a: np.ascontiguousarray(np.asarray(a, dtype=np.float32))
    ewin = f(inp["even_w_in"])
    win_e = np.stack([tile_w(ewin[e][:, :3072]) for e in range(2)])
    wfg = np.zeros((2, 1024, 128), np.float32)
    for e in range(2):
        for g in range(4):
            wfg[e][:, g * 32:g * 32 + 8] = ewin[e][:, 3072:3080]
    wfg_t = np.stack([tile_w(wfg[e])[0] for e in range(2)])
    dwin = f(inp["diff_w_in"])
    win_d = np.stack([tile_w(dwin[o]) for o in range(2)])
    wg = np.stack([tile_w(f(inp["ffn_w_gate"])[l]) for l in range(4)])
    wu = np.stack([tile_w(f(inp["ffn_w_up"])[l]) for l in range(4)])
    wd = np.stack([tile_w(f(inp["ffn_w_down"])[l]) for l in range(4)])
    inp32 = {k: f(v) for k, v in inp.items() if k != "x"}
    return dict(params=build_params(inp32), cmat=build_cmat(), win_e=win_e, wfg=wfg_t,
                wout_e=f(inp["even_w_out"]), win_d=win_d, wout_d=f(inp["diff_w_out"]), wg=wg, wu=wu, wd=wd)


_NC_CACHE = {}


def kernel(**inputs):
    x = np.asarray(inputs["x"], dtype=np.float32)
    B, S, _ = x.shape
    ncores = 8
    nseq = B // ncores
    key = (S, nseq)
    if key not in _NC_CACHE:
        _NC_CACHE[key] = build_program(S=S, NSEQ=nseq, LAYERS=4)
    nc = _NC_CACHE[key]
    shared = make_shared_inputs(inputs)
    in_maps = []
    for c in range(ncores):
        xs = x[c * nseq:(c + 1) * nseq]
        xT = np.ascontiguousarray(xs.transpose(0, 2, 1)).reshape(nseq, 8, 128, S)
        m = dict(shared)
        m["xT"] = xT
        in_maps.append(m)
    res = run_bass_kernel_spmd(nc, in_maps, core_ids=list(range(ncores)))
    out = np.empty((B, S, D), np.float32)
    for c in range(ncores):
        yT = np.asarray(res.results[c]["yT"]).reshape(nseq, D, S)
        out[c * nseq:(c + 1) * nseq] = yT.transpose(0, 2, 1)
    return out
```

```python
import math
import numpy as np
from contextlib import ExitStack
import concourse.bass as bass
import concourse.mybir as mybir
from concourse.bass_utils import run_bass_kernel_spmd

F32 = mybir.dt.float32
BF16 = mybir.dt.bfloat16
AF = mybir.ActivationFunctionType
ALU = mybir.AluOpType
AX = mybir.AxisListType

D = 1024
HD = 64
DFF = 2816
NJ = DFF // 128
NBK = 32
EPS = 1e-6
NEG = -30000.0
ENGS = ("pe", "act", "dve", "pool", "sp")


class _Op:
    __slots__ = ("idx", "eng", "emit", "deps", "dma", "marked", "ms", "epoch")

    def __init__(self, idx, eng, emit, dma, epoch):
        self.idx = idx
        self.eng = eng
        self.emit = emit
        self.dma = dma
        self.deps = ()
        self.marked = False
        self.ms = None
        self.epoch = epoch


class Sched:
    def __init__(self, nc, stack, n_dma_sems=6):
        self.nc = nc
        self.stack = stack
        self.ops = []
        self.last_w = {}
        self.readers = {}
        self.epoch = 0
        self.n_dma_sems = n_dma_sems
        self.eng_ops = {e: [] for e in ENGS}

    def new_epoch(self):
        self.epoch += 1

    def add(self, eng, emit, reads=(), writes=(), dma=False):
        op = _Op(len(self.ops), eng, emit, dma, self.epoch)
        deps = set()
        for k in reads:
            w = self.last_w.get(k)
            if w is not None:
                deps.add(w)
        for k in writes:
            w = self.last_w.get(k)
            if w is not None:
                deps.add(w)
            for r in self.readers.get(k, ()):
                deps.add(r)
        if eng == "pe" and not dma:
            deps = {d for d in deps if not (d.eng == "pe" and not d.dma)}
        op.deps = deps
        for k in reads:
            self.readers.setdefault(k, []).append(op)
        for k in writes:
            self.last_w[k] = op
            self.readers[k] = []
        self.eng_ops[eng].append(op)
        self.ops.append(op)
        return op

    def handoff(self, from_keys, to_keys):
        users = []
        for k in from_keys:
            w = self.last_w.get(k)
            if w is not None:
                users.append(w)
            users.extend(self.readers.get(k, ()))
        best = {}
        keep = []
        for u in users:
            if u.dma:
                keep.append(u)
            else:
                b = best.get(u.eng)
                if b is None or b.idx < u.idx:
                    best[u.eng] = u
        keep.extend(best.values())
        for k in to_keys:
            self.readers.setdefault(k, []).extend(keep)

    def finalize(self, final_waits=()):
        nc = self.nc
        for op in self.ops:
            for d in op.deps:
                d.marked = True
        for op in final_waits:
            op.marked = True
        sems = {}
        cnt = {}
        dma_cnt = {e: 0 for e in ENGS}

        def getsem(key):
            if key not in sems:
                sems[key] = self.stack.enter_context(nc.semaphore("s_%s_%s" % key))
                cnt[key] = 0
            return sems[key]

        for e in ENGS:
            for op in self.eng_ops[e]:
                if op.dma:
                    n = dma_cnt[e]
                    dma_cnt[e] += 1
                    key = ("d" + e, n % self.n_dma_sems)
                    s = getsem(key)
                    cnt[key] += 16
                    op.ms = (s, cnt[key])
                elif op.marked:
                    key = (e, op.epoch)
                    s = getsem(key)
                    cnt[key] += 1
                    op.ms = (s, cnt[key])
        self.n_sems = len(sems)
        self.max_cnt = max(cnt.values()) if cnt else 0

        def run_engine(e, engobj):
            waited = {}
            for op in self.eng_ops[e]:
                need = {}
                for d in op.deps:
                    s, v = d.ms
                    k = id(s)
                    if k not in need or need[k][1] < v:
                        need[k] = (s, v)
                if op.dma:
                    s, v = op.ms
                    if v > 16:
                        k = id(s)
                        if k not in need or need[k][1] < v - 16:
                            need[k] = (s, v - 16)
                pend = []
                for k, (s, v) in need.items():
                    if waited.get(k, 0) >= v:
                        continue
                    pend.append((s, v))
                    waited[k] = v
                attach = pend.pop() if (pend and not op.dma) else None
                for (s, v) in pend:
                    engobj.wait_ge(s, v)
                ins = op.emit(engobj)
                if attach is not None:
                    ins._wait_ge(attach[0], attach[1])
                if op.ms is not None:
                    ins.then_inc(op.ms[0], 16 if op.dma else 1)
            if e == "sp":
                for op in final_waits:
                    engobj.wait_ge(op.ms[0], op.ms[1])

        with nc.Block() as block:
            @block.tensor
            def _(eng):
                run_engine("pe", eng)

            @block.scalar
            def _(eng):
                run_engine("act", eng)

            @block.vector
            def _(eng):
                run_engine("dve", eng)

            @block.gpsimd
            def _(eng):
                run_engine("pool", eng)

            @block.sync
            def _(eng):
                run_engine("sp", eng)


def param_layout():
    off = {}
    n = 0
    for name, w in (("g_attn", 32), ("g_ffn", 32), ("fox_gq", 2), ("fox_gk", 2), ("fox_b", 2),
                    ("diff_gq", 2), ("diff_gk", 2), ("subln", 2),
                    ("lq1", 128), ("lk1", 128), ("lq2", 128), ("lk2", 128), ("relb", 256),
                    ("m1", 1), ("m2", 1), ("selA", 8), ("selB", 8), ("bk", 256), ("mask0", 256)):
        off[name] = n
        n += w
    return off, n


def t5_bucket_np(dist):
    max_exact = NBK // 2
    nf = np.maximum(dist, 1).astype(np.float32)
    large = max_exact + (np.log(nf / max_exact) / math.log(128 / max_exact) * (NBK - max_exact)).astype(np.int32)
    large = np.minimum(large, NBK - 1)
    return np.where(dist < max_exact, dist, large)


def tile_w(W):
    K, N = W.shape
    t = W.reshape(K // 128, 128, N // 128, 128).transpose(2, 1, 0, 3)
    return np.ascontiguousarray(t).reshape(N // 128, 128, K)


def build_params(inp):
    off, n = param_layout()
    P = np.zeros((128, n), np.float32)
    p = np.arange(128)
    P[:, off["g_attn"]:off["g_attn"] + 32] = inp["attn_norm_g"].reshape(4, 8, 128).transpose(2, 0, 1).reshape(128, 32)
    P[:, off["g_ffn"]:off["g_ffn"] + 32] = inp["ffn_norm_g"].reshape(4, 8, 128).transpose(2, 0, 1).reshape(128, 32)
    P[:, off["fox_gq"]:off["fox_gq"] + 2] = inp["fox_q_norm_g"][:, p % 64].T
    P[:, off["fox_gk"]:off["fox_gk"] + 2] = inp["fox_k_norm_g"][:, p % 64].T
    fb = np.zeros((128, 2), np.float32)
    for g in range(4):
        fb[g * 32:g * 32 + 8, :] = inp["fox_forget_b"].T
    P[:, off["fox_b"]:off["fox_b"] + 2] = fb
    P[:, off["diff_gq"]:off["diff_gq"] + 2] = inp["diff_q_norm_g"][:, p % 64].T
    P[:, off["diff_gk"]:off["diff_gk"] + 2] = inp["diff_k_norm_g"][:, p % 64].T
    P[:, off["subln"]:off["subln"] + 2] = inp["diff_subln_g"].T
    for nm, key in (("lq1", "diff_lambda_q1"), ("lk1", "diff_lambda_k1"), ("lq2", "diff_lambda_q2"), ("lk2", "diff_lambda_k2")):
        P[:, off[nm]:off[nm] + 128] = np.broadcast_to(inp[key].reshape(1, 128), (128, 128))
    P[:, off["relb"]:off["relb"] + 256] = np.broadcast_to(inp["rel_bias"].reshape(1, 256), (128, 256))
    m1 = np.zeros(128, np.float32)
    m2 = np.zeros(128, np.float32)
    m1[0:8] = -1.0
    m1[64:72] = 1.0
    m2[32:40] = -1.0
    m2[96:104] = 1.0
    P[:, off["m1"]] = m1
    P[:, off["m2"]] = m2
    for h in range(8):
        a = np.zeros(128, np.float32)
        b = np.zeros(128, np.float32)
        a[h] = 1.0
        a[32 + h] = 1.0
        b[64 + h] = 1.0
        b[96 + h] = 1.0
        P[:, off["selA"] + h] = a
        P[:, off["selB"] + h] = b
    s = np.arange(128)[:, None]
    t = np.arange(128)[None, :]
    d0 = t - s
    d1 = t - s + 128
    bk = np.concatenate([t5_bucket_np(np.maximum(d0, 0)), t5_bucket_np(d1)], axis=1).astype(np.float32)
    P[:, off["bk"]:off["bk"] + 256] = bk
    mask0 = np.zeros((128, 256), np.float32)
    mask0[:, 0:128] = np.where(s > t, NEG, 0.0)
    P[:, off["mask0"]:off["mask0"] + 256] = mask0
    return P


def build_cmat():
    j = np.arange(128)[:, None]
    s = np.arange(128)[None, :]
    ident = (j == s).astype(np.float32)
    negtri = np.where(j >= s, -1.0, 0.0).astype(np.float32)
    negones = -np.ones((128, 128), np.float32)
    ones = np.ones((128, 128), np.float32)
    blk = ((j // 64) == (s // 64)).astype(np.float32)
    maskS = np.where(j >= s, NEG, 0.0).astype(np.float32)
    maskI = np.where(j > s, NEG, 0.0).astype(np.float32)
    return np.concatenate([ident, negtri, negones, ones, blk, maskS, maskI], axis=1)


C_ID, C_NTRI, C_NONES, C_ONES, C_BLK, C_MS, C_MI = range(7)


def build_program(S=2048, NSEQ=2, LAYERS=4):
    assert S % 512 == 0
    NTC = S // 512
    NKB = S // 128
    nc = bass.Bass("TRN2", target_bir_lowering=False)
    poff, NP = param_layout()

    def din(name, shape):
        return nc.dram_tensor(name, list(shape), F32, kind="ExternalInput").ap()

    xT_d = din("xT", (NSEQ, 8, 128, S))
    params_d = din("params", (128, NP))
    cmat_d = din("cmat", (128, 7 * 128))
    win_e_d = din("win_e", (2, 24, 128, 1024))
    wfg_d = din("wfg", (2, 128, 1024))
    wout_e_d = din("wout_e", (2, 1024, 1024))
    win_d_d = din("win_d", (2, 24, 128, 1024))
    wout_d_d = din("wout_d", (2, 1024, 1024))
    wg_d = din("wg", (4, NJ, 128, 1024))
    wu_d = din("wu", (4, NJ, 128, 1024))
    wd_d = din("wd", (4, 8, 128, DFF))
    yT_d = nc.dram_tensor("yT", [NSEQ, 8, 128, S], F32, kind="ExternalOutput").ap()

    st = ExitStack()
    with st:
        Sc = Sched(nc, st)
        add = Sc.add

        def sb(name, n, dt):
            return st.enter_context(nc.sbuf_tensor("sb_" + name, [128, n], dt))

        xT = sb("xT", 8 * S, F32)
        hT = sb("hT", 8 * S, BF16)
        prm = sb("prm", NP, F32)
        cm = sb("cm", 7 * 128, BF16)
        rs2 = sb("rs2", 512, F32)
        sq2 = sb("sq2", 512, BF16)
        biasT = sb("biasT", 8 * 256, BF16)
        small = sb("small", 64, F32)
        C_EPS, C_ONE, C_NEGB0, C_NEGB1, C_GQ8F0, C_GQ8F1, C_GQ8D0, C_GQ8D1 = range(8)
        C_NLAM = 8
        C_GS = 10
        C_TMP = 12
        A16N = 38912
        A32N = 4864
        a16 = sb("a16", A16N, BF16)
        a32 = sb("a32", A32N, F32)
        ps = st.enter_context(nc.psum_tensor("ps", [128, 4096], F32))

        def bank(b):
            return ps[:, b * 512:(b + 1) * 512]

        o = 0

        def carve(n):
            nonlocal o
            r = o
            o += n
            return r

        OTC = [carve(S), carve(S)]
        QT = [carve(S), carve(S)]
        KT = [carve(S), carve(S)]
        VV = [carve(NKB * 128), carve(NKB * 128)]
        QH = [carve(S), carve(S)]
        KH = [carve(S), carve(S)]
        CALL = carve(S)
        KP2 = [QH[1], KH[1]]
        WSLOT = [carve(1024) for _ in range(5)]
        LP = [carve(512), carve(512)]
        LSUM = [carve(512), carve(512)]
        WT = [carve(512) for _ in range(4)]
        SQ = [carve(512), carve(512)]
        HI = carve(512)
        LO = carve(512)
        TMPB = carve(512)
        ATT_END = o
        assert ATT_END <= A16N, ATT_END
        o = 0
        GT = carve(NJ * 1024)
        FW = [carve(1024) for _ in range(6)]
        WD = [carve(DFF), carve(DFF)]
        SQF = [carve(512), carve(512)]
        assert o <= A16N, o
        o = 0
        EB = [carve(512), carve(512)]
        RS = [carve(512), carve(512)]
        LG = carve(512)
        CP = [carve(512), carve(512)]
        T1 = carve(512)
        OD = carve(256)
        RD = carve(512)
        assert o <= A32N, o
        STM = [EB[0], EB[1]]

        def A(off, n):
            return a16[:, off:off + n]

        def A3(off, n):
            return a32[:, off:off + n]

        def cmat(i):
            return cm[:, i * 128:(i + 1) * 128]

        def pcol(name, j=0):
            c = poff[name] + j
            return prm[:, c:c + 1]

        def scol(j):
            return small[:, j:j + 1]

        ARENA_ATT = ["att16"]
        ARENA_FFN = ["ffn16"]

        def mm(out, lhsT, rhs, start, stop, reads, writes, skip=False):
            return add("pe", lambda e: e.matmul(out, lhsT=lhsT, rhs=rhs, start=start, stop=stop,
                                                skip_group_check=skip), reads, writes)

        def act(out, in_, func, reads, writes, bias=None, scale=None):
            kw = {}
            if bias is not None:
                kw["bias"] = bias
            if scale is not None:
                kw["scale"] = scale
            return add("act", lambda e: e.activation(out=out, in_=in_, func=func, **kw), reads, writes)

        def dma(eng, out, in_, reads, writes):
            return add(eng, lambda e: e.dma_start(out=out, in_=in_), reads, writes, dma=True)

        dma("sp", prm[:, :], params_d, [], ["prm"])
        dma("pool", cm[:, :], cmat_d, [], ["cm"])
        add("dve", lambda e: e.memset(small[:, C_EPS:C_EPS + 1], EPS), [], ["small"])
        add("dve", lambda e: e.memset(small[:, C_ONE:C_ONE + 1], 1.0), [], ["small"])
        for e_ in range(2):
            add("dve", lambda e, e_=e_: e.tensor_scalar(out=small[:, C_NEGB0 + e_:C_NEGB0 + e_ + 1], in0=pcol("fox_b", e_),
                                                        scalar1=-1.0, scalar2=None, op0=ALU.mult),
                ["prm", "small"], ["small"])
            add("dve", lambda e, e_=e_: e.tensor_scalar(out=small[:, C_GQ8F0 + e_:C_GQ8F0 + e_ + 1], in0=pcol("fox_gq", e_),
                                                        scalar1=0.125, scalar2=None, op0=ALU.mult),
                ["prm", "small"], ["small"])
            add("dve", lambda e, e_=e_: e.tensor_scalar(out=small[:, C_GQ8D0 + e_:C_GQ8D0 + e_ + 1], in0=pcol("diff_gq", e_),
                                                        scalar1=0.125, scalar2=None, op0=ALU.mult),
                ["prm", "small"], ["small"])

        gen_state = {"banks": [7], "i": 0}

        def gbank():
            b = gen_state["banks"][gen_state["i"] % len(gen_state["banks"])]
            gen_state["i"] += 1
            return b

        wslot_i = [0]

        def load_w(src_ap, n=1024, slots=WSLOT, ctr=wslot_i, tag="ws"):
            i = ctr[0] % len(slots)
            ctr[0] += 1
            key = (tag, i)
            dma("pool", A(slots[i], n), src_ap, [], [key])
            return slots[i], key

        def norm(gname, l, sqbufs):
            for tc in range(NTC):
                nb = 7
                for c in range(8):
                    sq = sqbufs[c % 2]
                    xs = xT[:, c * S + tc * 512: c * S + tc * 512 + 512]
                    if c % 2 == 0:
                        add("pool", lambda e, sq=sq, xs=xs: e.tensor_tensor(out=A(sq, 512), in0=xs, in1=xs, op=ALU.mult),
                            [("xT", c, tc)], [("sq", sq)])
                    else:
                        act(A(sq, 512), xs, AF.Square, [("xT", c, tc)], [("sq", sq)])
                    mm(bank(nb), cmat(C_ONES), A(sq, 512), c == 0, c == 7, [("sq", sq), "cm"], [("ps", nb)])
                act(A3(RS[0], 512), bank(nb), AF.Ln, [("ps", nb), "small"], [("rs", 0)], bias=scol(C_EPS), scale=1.0 / D)
                act(A3(RS[1], 512), A3(RS[0], 512), AF.Exp, [("rs", 0)], [("rs", 1)], scale=-0.5)
                for c in range(8):
                    xs = xT[:, c * S + tc * 512: c * S + tc * 512 + 512]
                    hs = hT[:, c * S + tc * 512: c * S + tc * 512 + 512]
                    gc = pcol(gname, l * 8 + c)
                    add("dve", lambda e, xs=xs, hs=hs, gc=gc: e.scalar_tensor_tensor(
                        out=hs, in0=xs, scalar=gc, in1=A3(RS[1], 512), op0=ALU.mult, op1=ALU.mult),
                        [("xT", c, tc), ("rs", 1), "prm"], [("hT", c, tc)])
                yield "tc"

        def adv(gen):
            for v in gen:
                if v == "tc":
                    return

        def proj_feat(wkey, woff, dst_off, mode, gcol=None, dst2_off=None):
            for tc in range(NTC):
                b = gbank()
                for kc in range(8):
                    mm(bank(b), A(woff, 1024)[:, kc * 128:(kc + 1) * 128], hT[:, kc * S + tc * 512: kc * S + tc * 512 + 512],
                       kc == 0, kc == 7, [wkey, ("hT", kc, tc)], [("ps", b)])
                yield
                if dst2_off is None:
                    parts = [(slice(0, 128), dst_off)]
                else:
                    parts = [(slice(0, 64), dst_off), (slice(64, 128), dst2_off)]
                if mode == "qknorm":
                    sq = SQ[tc % 2]
                    act(A(sq, 512), bank(b), AF.Square, [("ps", b)], [("sq", sq)])
                    b2 = gbank()
                    mm(bank(b2), cmat(C_BLK), A(sq, 512), True, True, [("sq", sq), "cm"], [("ps", b2)])
                    yield
                    act(A3(RS[0], 512), bank(b2), AF.Ln, [("ps", b2), "small"], [("rs", 0)], bias=scol(C_EPS), scale=1.0 / HD)
                    act(A3(RS[1], 512), A3(RS[0], 512), AF.Exp, [("rs", 0)], [("rs", 1)], scale=-0.5)
                    yield
                for (rs_, doff) in parts:
                    dst = A(doff, S)[rs_, tc * 512:(tc + 1) * 512]
                    src = bank(b)[rs_, :]
                    dkey = ("buf", doff, tc)
                    if mode == "scale8":
                        add("dve", lambda e, dst=dst, src=src: e.tensor_scalar(out=dst, in0=src, scalar1=0.125, scalar2=None,
                                                                               op0=ALU.mult), [("ps", b)], [dkey])
                    elif mode == "copy":
                        add("dve", lambda e, dst=dst, src=src: e.tensor_copy(out=dst, in_=src), [("ps", b)], [dkey])
                    else:
                        add("dve", lambda e, dst=dst, src=src, rs_=rs_, gcol=gcol: e.scalar_tensor_tensor(
                            out=dst, in0=src, scalar=gcol[rs_, :], in1=A3(RS[1], 512)[rs_, :], op0=ALU.mult, op1=ALU.mult),
                            [("ps", b), ("rs", 1), "prm", "small"], [dkey])
                yield "tc"

        def proj_v(wkey, woff, dst_off):
            for g in range(NKB // 4):
                b = gbank()
                for i in range(4):
                    kb = 4 * g + i
                    for kc in range(8):
                        mm(bank(b)[:, i * 128:(i + 1) * 128], hT[:, kc * S + kb * 128: kc * S + kb * 128 + 128],
                           A(woff, 1024)[:, kc * 128:(kc + 1) * 128], kc == 0, kc == 7,
                           [wkey, ("hT", kc, kb // 4)], [("ps", b)], skip=True)
                    if i % 2 == 1:
                        yield
                dst = A(dst_off, NKB * 128)[:, g * 512:(g + 1) * 512]
                add("dve", lambda e, dst=dst, b=b: e.tensor_copy(out=dst, in_=bank(b)), [("ps", b)], [("buf", dst_off, g)])
                yield "tc"

        def out_proj_partial(w_rows_ap, otc_off):
            woff, wkey = load_w(w_rows_ap)
            for m in range(8):
                for tc in range(NTC):
                    b = gbank()
                    mm(bank(b), A(woff, 1024)[:, m * 128:(m + 1) * 128], A(otc_off, S)[:, tc * 512:(tc + 1) * 512],
                       True, True, [wkey, ("buf", otc_off, tc)], [("ps", b)])
                    xs = xT[:, m * S + tc * 512: m * S + tc * 512 + 512]
                    add("dve", lambda e, xs=xs, b=b: e.tensor_tensor(out=xs, in0=xs, in1=bank(b), op=ALU.add),
                        [("ps", b), ("xT", m, tc)], [("xT", m, tc)])
                    if tc % 2 == 1:
                        yield

        bgq = []

        def pump(n=1):
            for _ in range(n):
                while bgq:
                    try:
                        next(bgq[0])
                        break
                    except StopIteration:
                        bgq.pop(0)

        def drain():
            while bgq:
                pump()

        def run(gen):
            for _ in gen:
                pass

        def vkeys(voff):
            return [("buf", voff, g) for g in range(NKB // 4)]

        def tkeys(off):
            return [("buf", off, tc) for tc in range(NTC)]

        ones32 = A3(LG, 512)[:, 0:128]

        def wsum_bufs(qc):
            if qc % 2 == 0:
                return (EB[0], ("eb", 0)), (EB[1], ("eb", 1))
            return (CP[0], ("cp", 0)), (CP[1], ("cp", 1))

        def den_acc(n, qc, q0, w, wkey, two_maps=False):
            boff, bkey = wsum_bufs(qc)[n % 2]
            eng = "dve"
            full = A3(boff, 512)
            if two_maps:
                v = full.rearrange("p (m q) -> p m q", m=2)
                dstv = v[:, :, q0:256]
                zerov = v[:, :, 0:q0] if q0 > 0 else None
            else:
                dstv = full[:, q0:512]
                zerov = full[:, 0:q0] if q0 > 0 else None
            if n < 2:
                if zerov is not None:
                    add(eng, lambda e: e.memset(zerov, 0.0), [], [bkey])
                add(eng, lambda e: e.tensor_copy(out=dstv, in_=w), [wkey], [bkey])
            else:
                add(eng, lambda e: e.tensor_tensor(out=dstv, in0=dstv, in1=w, op=ALU.add), [wkey, bkey], [bkey])

        def sb_head(qoff, koff, voff, otc_off, pofs, ctr):
            pairs = []
            for qc in range(NTC):
                kbs = list(range(min(4 * qc + 3, NKB - 1), -1, -1))
                for n, kb in enumerate(kbs):
                    pairs.append(dict(qc=qc, kb=kb, q0=max(0, 128 * (kb - 4 * qc)), first=(n == 0), last=(kb == 0), n=n))
            NZ = 3
            qk_r = tkeys(qoff) + tkeys(koff)

            def zb(i):
                return i % NZ

            def stA(i):
                p = pairs[i]
                q0 = p["q0"]
                z = bank(zb(i))
                diag = p["kb"] >= 4 * p["qc"]
                mm(z[:, q0:512], A(koff, S)[:, p["kb"] * 128:(p["kb"] + 1) * 128],
                   A(qoff, S)[:, p["qc"] * 512 + q0:(p["qc"] + 1) * 512], True, not diag,
                   qk_r, [("ps", zb(i))], skip=True)
                if diag:
                    mm(z[:, q0:q0 + 128], cmat(C_ID), cmat(C_MS), False, True, ["cm"], [("ps", zb(i))], skip=True)

            def stB(i):
                p = pairs[i]
                q0 = p["q0"]
                act(A3(EB[i % 2], 512)[:, q0:512], bank(zb(i))[:, q0:512], AF.Exp, [("ps", zb(i))], [("eb", i % 2)])

            def stC(i):
                p = pairs[i]
                q0 = p["q0"]
                act(A(LP[i % 2], 512)[:, q0:512], A3(EB[i % 2], 512)[:, q0:512], AF.Ln, [("eb", i % 2), "small"],
                    [("lp", i % 2)], bias=scol(C_ONE))

            def stD(i):
                p = pairs[i]
                q0 = p["q0"]
                z = bank(zb(i))
                if p["first"]:
                    for k in range(2):
                        add("pool", lambda e, k=k: e.memset(A(LSUM[k], 512), 0.0), [], [("lsum", k)])
                cur = p["n"] % 2
                prv = 1 - cur
                mm(z[:, q0:512], cmat(C_NTRI), A(LP[i % 2], 512)[:, q0:512], False, p["first"],
                   [("lp", i % 2), "cm", ("ps", zb(i))], [("ps", zb(i))], skip=True)
                if not p["first"]:
                    mm(z[:, q0:512], cmat(C_NONES), A(LSUM[prv], 512)[:, q0:512], False, True,
                       [("lsum", prv), "cm", ("ps", zb(i))], [("ps", zb(i))], skip=True)
                if not p["last"]:
                    lpi = LP[i % 2]
                    if p["first"]:
                        add("pool", lambda e, lpi=lpi, cur=cur, q0=q0: e.tensor_copy(out=A(LSUM[cur], 512)[:, q0:512],
                                                                                     in_=A(lpi, 512)[:, q0:512]),
                            [("lp", i % 2)], [("lsum", cur)])
                    else:
                        add("pool", lambda e, lpi=lpi, cur=cur, prv=prv, q0=q0: e.tensor_tensor(
                            out=A(LSUM[cur], 512)[:, q0:512], in0=A(LSUM[prv], 512)[:, q0:512], in1=A(lpi, 512)[:, q0:512],
                            op=ALU.add), [("lp", i % 2), ("lsum", prv)], [("lsum", cur)])

            def stE(i):
                p = pairs[i]
                q0 = p["q0"]
                act(A(WT[i % 4], 512)[:, q0:512], bank(zb(i))[:, q0:512], AF.Exp, [("ps", zb(i))], [("wt", i % 4)])

            def stF(i):
                p = pairs[i]
                q0 = p["q0"]
                if p["first"]:
                    ctr[0] += 1
                ob = 3 + ctr[0] % 2
                mm(bank(ob)[pofs:pofs + 64, q0:512], A(voff, NKB * 128)[:, p["kb"] * 128 + pofs: p["kb"] * 128 + pofs + 64],
                   A(WT[i % 4], 512)[:, q0:512], p["first"], p["last"], [("wt", i % 4)] + vkeys(voff), [("ps", ob)], skip=True)
                if p["last"]:
                    dst = A(otc_off, S)[pofs:pofs + 64, p["qc"] * 512:(p["qc"] + 1) * 512]
                    add("dve", lambda e, dst=dst, ob=ob: e.tensor_copy(out=dst, in_=bank(ob)[pofs:pofs + 64, :]),
                        [("ps", ob)], [("buf", otc_off, p["qc"])])

            n = len(pairs)
            for t in range(-1, n + 2):
                if 0 <= t < n:
                    stB(t)
                if 0 <= t - 2 < n:
                    stE(t - 2)
                if 0 <= t + 1 < n:
                    stA(t + 1)
                if 0 <= t < n:
                    stC(t)
                    stD(t)
                if 0 <= t - 2 < n:
                    stF(t - 2)
                pump()

        def fox_head(qhoff, khoff, qoff, koff, voff, otc_off, pofs, ctr):
            pairs = []
            for qc in range(NTC):
                kbs = list(range(min(4 * qc + 3, NKB - 1), -1, -1))
                for n, kb in enumerate(kbs):
                    pairs.append(dict(qc=qc, kb=kb, q0=max(0, 128 * (kb - 4 * qc)), first=(n == 0), last=(kb == 0), n=n))
            ZB = [0, 1, 6]
            qk_r = tkeys(qoff) + tkeys(koff)

            def stA(i):
                p = pairs[i]
                q0 = p["q0"]
                z = bank(ZB[i % 3])
                diag = p["kb"] >= 4 * p["qc"]
                ks = slice(p["kb"] * 128, (p["kb"] + 1) * 128)
                qs = slice(p["qc"] * 512 + q0, (p["qc"] + 1) * 512)
                mm(z[:, q0:512], A(koff, S)[:, ks], A(qoff, S)[:, qs], True, False,
                   qk_r, [("ps", ZB[i % 3])], skip=True)
                mm(z[:, q0:512], A(khoff, S)[:, ks], A(qhoff, S)[:, qs], False, not diag,
                   [("aug", khoff), ("aug", qhoff)], [("ps", ZB[i % 3])], skip=True)
                if diag:
                    mm(z[:, q0:q0 + 128], cmat(C_ID), cmat(C_MI), False, True, ["cm"], [("ps", ZB[i % 3])], skip=True)

            def stB(i):
                p = pairs[i]
                q0 = p["q0"]
                act(A(WT[i % 4], 512)[:, q0:512], bank(ZB[i % 3])[:, q0:512], AF.Exp, [("ps", ZB[i % 3])], [("wt", i % 4)])

            def stF(i):
                p = pairs[i]
                q0 = p["q0"]
                if p["first"]:
                    ctr[0] += 1
                ob = 3 + ctr[0] % 2
                db = 5
                w = A(WT[i % 4], 512)[:, q0:512]
                mm(bank(ob)[pofs:pofs + 64, q0:512], A(voff, NKB * 128)[:, p["kb"] * 128 + pofs: p["kb"] * 128 + pofs + 64],
                   w, p["first"], p["last"], [("wt", i % 4)] + vkeys(voff), [("ps", ob)], skip=True)
                den_acc(p["n"], p["qc"], q0, w, ("wt", i % 4))
                if p["last"]:
                    (b0, k0), (b1, k1) = wsum_bufs(p["qc"])
                    mm(bank(db), ones32, A3(b0, 512), True, False, [k0, "lg"], [("ps", db)], skip=True)
                    mm(bank(db), ones32, A3(b1, 512), False, True, [k1, "lg"], [("ps", db)], skip=True)
                    rd = A3(RD, 512)[pofs:pofs + 64, :]
                    act(rd, bank(db)[pofs:pofs + 64, :], AF.Ln, [("ps", db)], ["rd"])
                    act(rd, rd, AF.Exp, ["rd"], ["rd"], scale=-1.0)
                    dst = A(otc_off, S)[pofs:pofs + 64, p["qc"] * 512:(p["qc"] + 1) * 512]
                    add("dve", lambda e, dst=dst, ob=ob, rd=rd: e.tensor_tensor(out=dst, in0=bank(ob)[pofs:pofs + 64, :],
                                                                               in1=rd, op=ALU.mult),
                        [("ps", ob), "rd"], [("buf", otc_off, p["qc"])])

            n = len(pairs)
            for t in range(-2, n):
                if 0 <= t + 2 < n:
                    stA(t + 2)
                if 0 <= t + 1 < n:
                    stB(t + 1)
                if 0 <= t < n:
                    stF(t)
                pump()

        def diff_head(h, qoff, koff, koff2, voff, otc_off, o_idx, ctr):
            NQ = S // 256
            pairs = []
            for qc in range(NQ):
                kbs = list(range(min(2 * qc + 1, NKB - 1), -1, -1))
                for n, kb in enumerate(kbs):
                    pairs.append(dict(qc=qc, kb=kb, q0=(128 if kb == 2 * qc + 1 else 0), first=(n == 0), last=(kb == 0), n=n))
            ZB = [0, 1, 6]
            qk_r = tkeys(qoff) + tkeys(koff) + tkeys(koff2)
            bdiag = biasT[:, h * 256: h * 256 + 128]
            boff = biasT[:, h * 256 + 128: h * 256 + 256]

            def v2(ap, q0):
                return ap.rearrange("p (m q) -> p m q", m=2)[:, :, q0:256]

            def stA(i):
                p = pairs[i]
                q0 = p["q0"]
                z = bank(ZB[i % 3])
                ks = slice(p["kb"] * 128, (p["kb"] + 1) * 128)
                qs = slice(p["qc"] * 256 + q0, (p["qc"] + 1) * 256)
                d = p["kb"] - 2 * p["qc"]
                near = d >= -1
                mm(z[:, q0:256], A(koff, S)[:, ks], A(qoff, S)[:, qs], True, False, qk_r, [("ps", ZB[i % 3])], skip=True)
                mm(z[:, 256 + q0:512], A(koff2, S)[:, ks], A(qoff, S)[:, qs], False, not near, qk_r,
                   [("ps", ZB[i % 3])], skip=True)
                if near:
                    blocks = []
                    if d == 1:
                        blocks = [(128, bdiag)]
                    elif d == 0:
                        blocks = [(0, bdiag), (128, boff)]
                    else:
                        blocks = [(0, boff)]
                    k = 0
                    for m_ in range(2):
                        for (c0, bt) in blocks:
                            k += 1
                            mm(z[:, m_ * 256 + c0: m_ * 256 + c0 + 128], cmat(C_ID), bt, False, k == 2 * len(blocks),
                               ["cm", "biasT"], [("ps", ZB[i % 3])], skip=True)

            def stB(i):
                p = pairs[i]
                q0 = p["q0"]
                act(v2(A(WT[i % 4], 512), q0), v2(bank(ZB[i % 3]), q0), AF.Exp, [("ps", ZB[i % 3])], [("wt", i % 4)])

            def stF(i):
                p = pairs[i]
                q0 = p["q0"]
                qc = p["qc"]
                if p["first"]:
                    ctr[0] += 1
                ob = 3 + ctr[0] % 2
                db = 5
                vk = A(voff, NKB * 128)[:, p["kb"] * 128:(p["kb"] + 1) * 128]
                if q0 == 0:
                    segs = [(0, 512)]
                else:
                    segs = [(q0, 256), (256 + q0, 512)]
                for si, (c0, c1) in enumerate(segs):
                    w = A(WT[i % 4], 512)[:, c0:c1]
                    mm(bank(ob)[:, c0:c1], vk, w, p["first"] and si == 0, p["last"] and si == len(segs) - 1,
                       [("wt", i % 4)] + vkeys(voff), [("ps", ob)], skip=True)
                den_acc(p["n"], qc, q0, v2(A(WT[i % 4], 512), q0), ("wt", i % 4), two_maps=True)
                if p["last"]:
                    exhaust_tails()
                    tailq.append(chunk_tail(ob, db, qc))

            tailq = []

            def pump_tail():
                if tailq:
                    try:
                        next(tailq[0])
                    except StopIteration:
                        tailq.pop(0)

            def exhaust_tails():
                while tailq:
                    pump_tail()

            def chunk_tail(ob, db, qc):
                (b0, k0), (b1, k1) = wsum_bufs(qc)
                mm(bank(db), ones32, A3(b0, 512), True, False, [k0, "lg"], [("ps", db)], skip=True)
                mm(bank(db), ones32, A3(b1, 512), False, True, [k1, "lg"], [("ps", db)], skip=True)
                yield
                act(A3(RD, 512), bank(db), AF.Ln, [("ps", db)], ["rd"])
                act(A3(RD, 512), A3(RD, 512), AF.Exp, ["rd"], ["rd"], scale=-1.0)
                yield
                add("dve", lambda e, ob=ob: e.tensor_tensor(out=A3(T1, 512), in0=bank(ob), in1=A3(RD, 512), op=ALU.mult),
                    [("ps", ob), "rd"], ["t1"])
                add("dve", lambda e: e.scalar_tensor_tensor(out=A3(OD, 256), in0=A3(T1, 512)[:, 256:512],
                                                            scalar=scol(C_NLAM + o_idx), in1=A3(T1, 512)[:, 0:256],
                                                            op0=ALU.mult, op1=ALU.add), ["t1", "small"], ["od"])
                yield
                sqa = sq2[:, (qc % 2) * 256:(qc % 2) * 256 + 256]
                sqk = ("sq2", qc % 2)
                add("pool", lambda e, sqa=sqa: e.tensor_tensor(out=sqa, in0=A3(OD, 256), in1=A3(OD, 256),
                                                               op=ALU.mult), ["od"], [sqk])
                yield
                nb = db
                mm(bank(nb)[:, 0:256], cmat(C_ONES), sqa, True, True, [sqk, "cm"], [("ps", nb)])
                yield
                act(rs2[:, 0:256], bank(nb)[:, 0:256], AF.Ln, [("ps", nb), "small"], [("rs2", 0)],
                    bias=scol(C_EPS), scale=1.0 / 128)
                act(rs2[:, 256:512], rs2[:, 0:256], AF.Exp, [("rs2", 0)], [("rs2", 1)], scale=-0.5)
                yield
                dst = A(otc_off, S)[:, qc * 256:(qc + 1) * 256]
                add("dve", lambda e, dst=dst: e.scalar_tensor_tensor(out=dst, in0=A3(OD, 256), scalar=scol(C_GS + o_idx),
                                                                     in1=rs2[:, 256:512], op0=ALU.mult, op1=ALU.mult),
                    ["od", ("rs2", 1), "small"], [("buf", otc_off, qc // 2)])

            n = len(pairs)
            for t in range(-2, n):
                if 0 <= t + 2 < n:
                    stA(t + 2)
                if 0 <= t + 1 < n:
                    stB(t + 1)
                if 0 <= t < n:
                    stF(t)
                pump_tail()
                pump()
            exhaust_tails()

        def even_layer(e_, ng):
            octr = [0]
            woff, wkey = load_w(wfg_d[e_])
            gen_state["banks"] = [5, 6, 7]

            def foxc(tc):
                b = gbank()
                for kc in range(8):
                    mm(bank(b), A(woff, 1024)[:, kc * 128:(kc + 1) * 128], hT[:, kc * S + tc * 512: kc * S + tc * 512 + 512],
                       kc == 0, kc == 7, [wkey, ("hT", kc, tc)], [("ps", b)])
                act(A3(EB[0], 512), bank(b), AF.Exp, [("ps", b), "small"], [("eb", 0)], bias=scol(C_NEGB0 + e_), scale=-1.0)
                act(A3(LG, 512), A3(EB[0], 512), AF.Ln, [("eb", 0), "small"], ["lg"], bias=scol(C_ONE))
                cpb = CP[tc % 2]
                init = 0.0 if tc == 0 else A3(CP[(tc - 1) % 2], 512)[:, 511:512]
                add("dve", lambda e, cpb=cpb, init=init: e.tensor_tensor_scan(
                    out=A3(cpb, 512), data0=small[:, C_ONE:C_ONE + 1].to_broadcast([128, 512]), data1=A3(LG, 512),
                    initial=init, op0=ALU.mult, op1=ALU.add),
                    ["lg", "small", ("cp", (tc - 1) % 2)], [("cp", tc % 2)])
                add("dve", lambda e, cpb=cpb: e.tensor_copy(out=A(HI, 512), in_=A3(cpb, 512)), [("cp", tc % 2)], ["hi"])
                add("dve", lambda e, cpb=cpb: e.tensor_tensor(out=A(LO, 512), in0=A3(cpb, 512), in1=A(HI, 512), op=ALU.subtract),
                    [("cp", tc % 2), "hi"], ["lo"])
                add("dve", lambda e: e.tensor_scalar(out=A(TMPB, 512), in0=A(HI, 512), scalar1=pcol("m1"), scalar2=None,
                                                     op0=ALU.mult), ["hi", "prm"], ["tmpb"])
                add("dve", lambda e, tc=tc: e.scalar_tensor_tensor(out=A(CALL, S)[:, tc * 512:(tc + 1) * 512], in0=A(LO, 512),
                                                                  scalar=pcol("m2"), in1=A(TMPB, 512), op0=ALU.mult, op1=ALU.add),
                    ["lo", "tmpb", "prm"], [("call", tc)])

            def pair_inproj(pi):
                fox = pi >= 4
                c = pi % 4
                buf = pi % 2
                jq, jk, jv = (12 + c, 16 + c, 20 + c) if fox else (c, 4 + c, 8 + c)
                wq, kq = load_w(win_e_d[e_, jq])
                wk, kk = load_w(win_e_d[e_, jk])
                wv, kv = load_w(win_e_d[e_, jv])
                if fox:
                    yield from proj_feat(kq, wq, QT[buf], "qknorm", scol(C_GQ8F0 + e_))
                    yield from proj_feat(kk, wk, KT[buf], "qknorm", pcol("fox_gk", e_), dst2_off=KP2[buf])
                else:
                    yield from proj_feat(kq, wq, QT[buf], "scale8")
                    yield from proj_feat(kk, wk, KT[buf], "copy", dst2_off=KP2[buf])
                yield from proj_v(kv, wv, VV[buf])

            gen_state["banks"] = [5, 6, 7]
            wq0, kq0 = load_w(win_e_d[e_, 0])
            wk0, kk0 = load_w(win_e_d[e_, 4])
            wv0, kv0 = load_w(win_e_d[e_, 8])
            g0 = [proj_feat(kq0, wq0, QT[0], "scale8"), proj_feat(kk0, wk0, KT[0], "copy", dst2_off=KP2[0]),
                  proj_v(kv0, wv0, VV[0])]
            for tc in range(NTC):
                adv(ng)
                foxc(tc)
                for g_ in g0:
                    adv(g_)
            for g_ in g0:
                run(g_)
            run(ng)
            add("dve", lambda e: e.memset(ones32, 1.0), [], ["lg"])
            for pi in range(8):
                fox = pi >= 4
                c = pi % 4
                buf = pi % 2
                gen_state["banks"] = [7, 2] if fox else [5, 6, 7]
                if pi >= 1:
                    bgq.append(out_proj_partial(wout_e_d[e_, (pi - 1) * 128:pi * 128, :], OTC[1 - buf]))
                if pi + 1 < 8:
                    bgq.append(pair_inproj(pi + 1))
                for hh in range(2):
                    pofs = 64 * hh
                    if fox:
                        h = 2 * c + hh
                        calls = [("call", tc) for tc in range(NTC)]
                        add("dve", lambda e, h=h: e.tensor_scalar(out=A(QH[0], S), in0=A(CALL, S), scalar1=pcol("selA", h),
                                                                  scalar2=pcol("selB", h), op0=ALU.mult, op1=ALU.add),
                            calls + ["prm"], [("aug", QH[0])])
                        add("dve", lambda e, h=h: e.tensor_scalar(out=A(KH[0], S), in0=A(CALL, S), scalar1=pcol("selB", h),
                                                                  scalar2=pcol("selA", h), op0=ALU.mult, op1=ALU.add),
                            calls + ["prm"], [("aug", KH[0])])
                        fox_head(QH[0], KH[0], QT[buf], (KT[buf], KP2[buf])[hh], VV[buf], OTC[buf], pofs, octr)
                    else:
                        sb_head(QT[buf], (KT[buf], KP2[buf])[hh], VV[buf], OTC[buf], pofs, octr)
                drain()
            run(out_proj_partial(wout_e_d[e_, 7 * 128:8 * 128, :], OTC[1]))

        def odd_layer(o_, l, first_time, ng):
            octr = [0]
            lam_init = 0.8 - 0.6 * math.exp(-0.3 * l)
            if first_time[0]:
                first_time[0] = False
                tb = A3(T1, 512)[:, 0:256]
                add("dve", lambda e: e.tensor_tensor(
                    out=tb.rearrange("p (b h) -> p b h", h=8),
                    in0=prm[:, poff["relb"]:poff["relb"] + 256].rearrange("p (b h) -> p b h", h=8),
                    in1=prm[:, poff["relb"] + 248:poff["relb"] + 256].unsqueeze(1).to_broadcast([128, 32, 8]),
                    op=ALU.subtract), ["prm"], ["t1"])
                acc = A3(RD, 512)[:, 0:256]
                tmp = A3(RD, 512)[:, 256:512]
                bkt = prm[:, poff["bk"]:poff["bk"] + 256]
                for h in range(8):
                    add("dve", lambda e: e.tensor_copy(out=acc, in_=prm[:, poff["mask0"]:poff["mask0"] + 256]), ["prm"], ["rd"])
                    for b_ in range(NBK):
                        col = A3(T1, 512)[:, b_ * 8 + h: b_ * 8 + h + 1]
                        add("dve", lambda e, col=col, b_=b_: e.tensor_scalar(out=tmp, in0=bkt, scalar1=float(b_), scalar2=col,
                                                                             op0=ALU.is_equal, op1=ALU.mult),
                            ["prm", "t1"], ["rdt"])
                        add("dve", lambda e: e.tensor_tensor(out=acc, in0=acc, in1=tmp, op=ALU.add), ["rd", "rdt"], ["rd"])
                    add("dve", lambda e, h=h: e.tensor_copy(out=biasT[:, h * 256:(h + 1) * 256], in_=acc), ["rd"], ["biasT"])
            for i_, (a_, b_) in enumerate((("lq1", "lk1"), ("lq2", "lk2"))):
                pa = prm[:, poff[a_] + o_ * 64: poff[a_] + o_ * 64 + 64]
                pb = prm[:, poff[b_] + o_ * 64: poff[b_] + o_ * 64 + 64]
                add("dve", lambda e, pa=pa, pb=pb: e.tensor_tensor(out=A3(OD, 256)[:, 0:64], in0=pa, in1=pb, op=ALU.mult),
                    ["prm"], ["od"])
                add("dve", lambda e, i_=i_: e.reduce_sum(out=small[:, C_TMP + i_:C_TMP + i_ + 1], in_=A3(OD, 256)[:, 0:64], axis=AX.X),
                    ["od"], ["small"])
                act(small[:, C_TMP + 2 + i_:C_TMP + 3 + i_], small[:, C_TMP + i_:C_TMP + i_ + 1], AF.Exp, ["small"], ["small"])
            add("dve", lambda e: e.tensor_tensor(out=small[:, C_TMP + 4:C_TMP + 5], in0=small[:, C_TMP + 3:C_TMP + 4],
                                                 in1=small[:, C_TMP + 2:C_TMP + 3], op=ALU.subtract), ["small"], ["small"])
            add("dve", lambda e: e.tensor_scalar(out=small[:, C_NLAM + o_:C_NLAM + o_ + 1], in0=small[:, C_TMP + 4:C_TMP + 5],
                                                 scalar1=-lam_init, scalar2=None, op0=ALU.add), ["small"], ["small"])
            add("dve", lambda e: e.tensor_scalar(out=small[:, C_GS + o_:C_GS + o_ + 1], in0=pcol("subln", o_),
                                                 scalar1=1.0 - lam_init, scalar2=None, op0=ALU.mult), ["small", "prm"], ["small"])
            gen_state["banks"] = [7, 2]
            add("dve", lambda e: e.memset(ones32, 1.0), [], ["lg"])

            def head_inproj(h):
                buf = h % 2
                wq, kq = load_w(win_d_d[o_, h])
                wk, kk = load_w(win_d_d[o_, 8 + h])
                wv, kv = load_w(win_d_d[o_, 16 + h])
                yield from proj_feat(kq, wq, QT[buf], "qknorm", scol(C_GQ8D0 + o_))
                yield from proj_feat(kk, wk, KT[buf], "qknorm", pcol("diff_gk", o_), dst2_off=KP2[buf])
                yield from proj_v(kv, wv, VV[buf])

            wq0, kq0 = load_w(win_d_d[o_, 0])
            wk0, kk0 = load_w(win_d_d[o_, 8])
            wv0, kv0 = load_w(win_d_d[o_, 16])
            g0 = [proj_feat(kq0, wq0, QT[0], "qknorm", scol(C_GQ8D0 + o_)),
                  proj_feat(kk0, wk0, KT[0], "qknorm", pcol("diff_gk", o_), dst2_off=KP2[0]),
                  proj_v(kv0, wv0, VV[0])]
            for tc in range(NTC):
                adv(ng)
                for g_ in g0:
                    adv(g_)
            for g_ in g0:
                run(g_)
            run(ng)
            for h in range(8):
                buf = h % 2
                if h >= 1:
                    bgq.append(out_proj_partial(wout_d_d[o_, (h - 1) * 128:h * 128, :], OTC[1 - buf]))
                if h + 1 < 8:
                    bgq.append(head_inproj(h + 1))
                diff_head(h, QT[buf], KT[buf], KP2[buf], VV[buf], OTC[buf], o_, octr)
                drain()
            run(out_proj_partial(wout_d_d[o_, 7 * 128:8 * 128, :], OTC[1]))

        fw_i = [0]
        wd_i = [0]

        def ffn(l, ng):
            NH = S // 1024 if S >= 1024 else 1
            TW = S // NH
            NT = TW // 512
            it = 0
            run(ng)
            for half in range(NH):
                for j in range(NJ):
                    wgo, kg = load_w(wg_d[l, j], slots=FW, ctr=fw_i, tag="fw")
                    wuo, ku = load_w(wu_d[l, j], slots=FW, ctr=fw_i, tag="fw")
                    for tl in range(NT):
                        tc = half * NT + tl
                        ba, bu = (0, 1) if it % 2 == 0 else (2, 3)
                        it += 1
                        for kc in range(8):
                            mm(bank(ba), A(wgo, 1024)[:, kc * 128:(kc + 1) * 128], hT[:, kc * S + tc * 512: kc * S + tc * 512 + 512],
                               kc == 0, kc == 7, [kg, ("hT", kc, tc)], [("ps", ba)])
                        for kc in range(8):
                            mm(bank(bu), A(wuo, 1024)[:, kc * 128:(kc + 1) * 128], hT[:, kc * S + tc * 512: kc * S + tc * 512 + 512],
                               kc == 0, kc == 7, [ku, ("hT", kc, tc)], [("ps", bu)])
                        stm = STM[it % 2]
                        act(A3(stm, 512), bank(ba), AF.Silu, [("ps", ba)], [("stm", stm)])
                        gdst = A(GT, NJ * 1024)[:, j * 1024 + tl * 512: j * 1024 + tl * 512 + 512]
                        add("dve", lambda e, gdst=gdst, stm=stm, bu=bu: e.tensor_tensor(out=gdst, in0=A3(stm, 512), in1=bank(bu),
                                                                                        op=ALU.mult),
                            [("stm", stm), ("ps", bu)], [("gT", j, tl)])
                for m in range(8):
                    wdo, kd = load_w(wd_d[l, m], n=DFF, slots=WD, ctr=wd_i, tag="wd")
                    for tl in range(NT):
                        tc = half * NT + tl
                        b = 4 + (m * NT + tl) % 2
                        for j in range(NJ):
                            mm(bank(b), A(wdo, DFF)[:, j * 128:(j + 1) * 128], A(GT, NJ * 1024)[:, j * 1024 + tl * 512: j * 1024 + tl * 512 + 512],
                               j == 0, j == NJ - 1, [kd, ("gT", j, tl)], [("ps", b)])
                        xs = xT[:, m * S + tc * 512: m * S + tc * 512 + 512]
                        add("dve", lambda e, xs=xs, b=b: e.tensor_tensor(out=xs, in0=xs, in1=bank(b), op=ALU.add),
                            [("ps", b), ("xT", m, tc)], [("xT", m, tc)])

        def att_keys():
            ks = []
            for off in OTC + QT + KT:
                ks += tkeys(off)
            for off in VV:
                ks += vkeys(off)
            for off in QH + KH:
                ks.append(("aug", off))
            for off in KP2:
                ks += tkeys(off)
            ks += [("call", tc) for tc in range(NTC)]
            ks += [("ws", i) for i in range(len(WSLOT))]
            ks += [("lp", 0), ("lp", 1), ("lsum", 0), ("lsum", 1), ("wt", 0), ("wt", 1), ("wt", 2), ("wt", 3)]
            ks += [("sq", SQ[0]), ("sq", SQ[1]), "hi", "lo", "tmpb"]
            return ks

        def ffn_keys():
            ks = [("gT", j, tl) for j in range(NJ) for tl in range(2)]
            ks += [("fw", i) for i in range(len(FW))] + [("wd", 0), ("wd", 1)]
            ks += [("sq", SQF[0]), ("sq", SQF[1])]
            return ks

        first_time = [True]
        outs = []
        for seq in range(NSEQ):
            for c in range(8):
                dma("sp", xT[:, c * S:(c + 1) * S], xT_d[seq, c], [], [("xT", c, tc) for tc in range(NTC)])
            for l in range(LAYERS):
                Sc.new_epoch()
                Sc.handoff(ffn_keys() + [("stm", STM[0]), ("stm", STM[1])], att_keys() + [("eb", 0), ("eb", 1)])
                for buf_ in range(2):
                    add("pool", lambda e, buf_=buf_: e.memset(A(KT[buf_], S)[64:128, :], 0.0), [], tkeys(KT[buf_]))
                    add("pool", lambda e, buf_=buf_: e.memset(A(KP2[buf_], S)[0:64, :], 0.0), [], tkeys(KP2[buf_]))
                ng = norm("g_attn", l, SQ)
                if l % 2 == 0:
                    even_layer(l // 2, ng)
                else:
                    odd_layer(l // 2, l, first_time, ng)
                Sc.handoff(att_keys() + [("eb", 0), ("eb", 1)], ffn_keys() + [("stm", STM[0]), ("stm", STM[1])])
                ffn(l, norm("g_ffn", l, SQF))
            for c in range(8):
                outs.append(dma("sp", yT_d[seq, c], xT[:, c * S:(c + 1) * S], [("xT", c, tc) for tc in range(NTC)], []))
        Sc.finalize(final_waits=outs)
        build_program.stats = dict(n_ops=len(Sc.ops), n_sems=Sc.n_sems, max_cnt=Sc.max_cnt,
                                   per_eng={e: len(v) for e, v in Sc.eng_ops.items()})
    return nc


def make_shared_inputs(inp):
    f = lambda a: np.ascontiguousarray(np.asarray(a, dtype=np.float32))
    ewin = f(inp["even_w_in"])
    win_e = np.stack([tile_w(ewin[e][:, :3072]) for e in range(2)])
    wfg = np.zeros((2, 1024, 128), np.float32)
    for e in range(2):
        for g in range(4):
            wfg[e][:, g * 32:g * 32 + 8] = ewin[e][:, 3072:3080]
    wfg_t = np.stack([tile_w(wfg[e])[0] for e in range(2)])
    dwin = f(inp["diff_w_in"])
    win_d = np.stack([tile_w(dwin[o]) for o in range(2)])
    wg = np.stack([tile_w(f(inp["ffn_w_gate"])[l]) for l in range(4)])
    wu = np.stack([tile_w(f(inp["ffn_w_up"])[l]) for l in range(4)])
    wd = np.stack([tile_w(f(inp["ffn_w_down"])[l]) for l in range(4)])
    inp32 = {k: f(v) for k, v in inp.items() if k != "x"}
    return dict(params=build_params(inp32), cmat=build_cmat(), win_e=win_e, wfg=wfg_t,
                wout_e=f(inp["even_w_out"]), win_d=win_d, wout_d=f(inp["diff_w_out"]), wg=wg, wu=wu, wd=wd)


_NC_CACHE = {}


def kernel(**inputs):
    x = np.asarray(inputs["x"], dtype=np.float32)
    B, S, _ = x.shape
    ncores = 8
    nseq = B // ncores
    key = (S, nseq)
    if key not in _NC_CACHE:
        _NC_CACHE[key] = build_program(S=S, NSEQ=nseq, LAYERS=4)
    nc = _NC_CACHE[key]
    shared = make_shared_inputs(inputs)
    in_maps = []
    for c in range(ncores):
        xs = x[c * nseq:(c + 1) * nseq]
        xT = np.ascontiguousarray(xs.transpose(0, 2, 1)).reshape(nseq, 8, 128, S)
        m = dict(shared)
        m["xT"] = xT
        in_maps.append(m)
    res = run_bass_kernel_spmd(nc, in_maps, core_ids=list(range(ncores)))
    out = np.empty((B, S, D), np.float32)
    for c in range(ncores):
        yT = np.asarray(res.results[c]["yT"]).reshape(nseq, D, S)
        out[c * nseq:(c + 1) * nseq] = yT.transpose(0, 2, 1)
    return out
```

```python
import math
import numpy as np
from contextlib import ExitStack
import concourse.bass as bass
import concourse.mybir as mybir
from concourse.bass_utils import run_bass_kernel_spmd

F32 = mybir.dt.float32
BF16 = mybir.dt.bfloat16
AF = mybir.ActivationFunctionType
ALU = mybir.AluOpType
AX = mybir.AxisListType

D = 1024
HD = 64
DFF = 2816
NJ = DFF // 128
NBK = 32
EPS = 1e-6
NEG = -30000.0
ENGS = ("pe", "act", "dve", "pool", "sp")


class _Op:
    __slots__ = ("idx", "eng", "emit", "deps", "dma", "marked", "ms", "epoch")

    def __init__(self, idx, eng, emit, dma, epoch):
        self.idx = idx
        self.eng = eng
        self.emit = emit
        self.dma = dma
        self.deps = ()
        self.marked = False
        self.ms = None
        self.epoch = epoch


class Sched:
    def __init__(self, nc, stack, n_dma_sems=6):
        self.nc = nc
        self.stack = stack
        self.ops = []
        self.last_w = {}
        self.readers = {}
        self.epoch = 0
        self.n_dma_sems = n_dma_sems
        self.eng_ops = {e: [] for e in ENGS}

    def new_epoch(self):
        self.epoch += 1

    def add(self, eng, emit, reads=(), writes=(), dma=False):
        op = _Op(len(self.ops), eng, emit, dma, self.epoch)
        deps = set()
        for k in reads:
            w = self.last_w.get(k)
            if w is not None:
                deps.add(w)
        for k in writes:
            w = self.last_w.get(k)
            if w is not None:
                deps.add(w)
            for r in self.readers.get(k, ()):
                deps.add(r)
        if eng == "pe" and not dma:
            deps = {d for d in deps if not (d.eng == "pe" and not d.dma)}
        op.deps = deps
        for k in reads:
            self.readers.setdefault(k, []).append(op)
        for k in writes:
            self.last_w[k] = op
            self.readers[k] = []
        self.eng_ops[eng].append(op)
        self.ops.append(op)
        return op

    def handoff(self, from_keys, to_keys):
        users = []
        for k in from_keys:
            w = self.last_w.get(k)
            if w is not None:
                users.append(w)
            users.extend(self.readers.get(k, ()))
        best = {}
        keep = []
        for u in users:
            if u.dma:
                keep.append(u)
            else:
                b = best.get(u.eng)
                if b is None or b.idx < u.idx:
                    best[u.eng] = u
        keep.extend(best.values())
        for k in to_keys:
            self.readers.setdefault(k, []).extend(keep)

    def finalize(self, final_waits=()):
        nc = self.nc
        for op in self.ops:
            for d in op.deps:
                d.marked = True
        for op in final_waits:
            op.marked = True
        sems = {}
        cnt = {}
        dma_cnt = {e: 0 for e in ENGS}

        def getsem(key):
            if key not in sems:
                sems[key] = self.stack.enter_context(nc.semaphore("s_%s_%s" % key))
                cnt[key] = 0
            return sems[key]

        for e in ENGS:
            for op in self.eng_ops[e]:
                if op.dma:
                    n = dma_cnt[e]
                    dma_cnt[e] += 1
                    key = ("d" + e, n % self.n_dma_sems)
                    s = getsem(key)
                    cnt[key] += 16
                    op.ms = (s, cnt[key])
                elif op.marked:
                    key = (e, op.epoch)
                    s = getsem(key)
                    cnt[key] += 1
                    op.ms = (s, cnt[key])
        self.n_sems = len(sems)
        self.max_cnt = max(cnt.values()) if cnt else 0

        know = {e: {} for e in ENGS}
        snap = {}
        plan = {}
        for op in self.ops:
            K = know[op.eng]
            need = {}
            for d in op.deps:
                s, v = d.ms
                k = id(s)
                if k not in need or need[k][1] < v:
                    need[k] = (s, v, d)
            if op.dma:
                s, v = op.ms
                if v > 16:
                    k = id(s)
                    if k not in need or need[k][1] < v - 16:
                        need[k] = (s, v - 16, None)
            pend = []
            for (s, v, d) in sorted(need.values(), key=lambda t: -(t[2].idx if t[2] is not None else -1)):
                k = id(s)
                if K.get(k, 0) >= v:
                    continue
                pend.append((s, v))
                K[k] = v
                if d is not None:
                    sn = snap.get(d)
                    if sn is not None:
                        for kk, vv in sn.items():
                            if K.get(kk, 0) < vv:
                                K[kk] = vv
            plan[op] = pend
            if op.ms is not None:
                sn = dict(K)
                k = id(op.ms[0])
                if sn.get(k, 0) < op.ms[1]:
                    sn[k] = op.ms[1]
                snap[op] = sn

        def run_engine(e, engobj):
            for op in self.eng_ops[e]:
                pend = list(plan[op])
                attach = pend.pop(0) if (pend and not op.dma) else None
                for (s, v) in pend:
                    engobj.wait_ge(s, v)
                ins = op.emit(engobj)
                if attach is not None:
                    ins._wait_ge(attach[0], attach[1])
                if op.ms is not None:
                    ins.then_inc(op.ms[0], 16 if op.dma else 1)
            if e == "sp":
                for op in final_waits:
                    engobj.wait_ge(op.ms[0], op.ms[1])

        with nc.Block() as block:
            @block.tensor
            def _(eng):
                run_engine("pe", eng)

            @block.scalar
            def _(eng):
                run_engine("act", eng)

            @block.vector
            def _(eng):
                run_engine("dve", eng)

            @block.gpsimd
            def _(eng):
                run_engine("pool", eng)

            @block.sync
            def _(eng):
                run_engine("sp", eng)


def param_layout():
    off = {}
    n = 0
    for name, w in (("g_attn", 32), ("g_ffn", 32), ("fox_gq", 2), ("fox_gk", 2), ("fox_b", 2),
                    ("diff_gq", 2), ("diff_gk", 2), ("subln", 2),
                    ("lq1", 128), ("lk1", 128), ("lq2", 128), ("lk2", 128), ("relb", 256),
                    ("m1", 1), ("m2", 1), ("selA", 8), ("selB", 8), ("bk", 256), ("mask0", 256)):
        off[name] = n
        n += w
    return off, n


def t5_bucket_np(dist):
    max_exact = NBK // 2
    nf = np.maximum(dist, 1).astype(np.float32)
    large = max_exact + (np.log(nf / max_exact) / math.log(128 / max_exact) * (NBK - max_exact)).astype(np.int32)
    large = np.minimum(large, NBK - 1)
    return np.where(dist < max_exact, dist, large)


def tile_w(W):
    K, N = W.shape
    t = W.reshape(K // 128, 128, N // 128, 128).transpose(2, 1, 0, 3)
    return np.ascontiguousarray(t).reshape(N // 128, 128, K)


def build_params(inp):
    off, n = param_layout()
    P = np.zeros((128, n), np.float32)
    p = np.arange(128)
    P[:, off["g_attn"]:off["g_attn"] + 32] = inp["attn_norm_g"].reshape(4, 8, 128).transpose(2, 0, 1).reshape(128, 32)
    P[:, off["g_ffn"]:off["g_ffn"] + 32] = inp["ffn_norm_g"].reshape(4, 8, 128).transpose(2, 0, 1).reshape(128, 32)
    P[:, off["fox_gq"]:off["fox_gq"] + 2] = inp["fox_q_norm_g"][:, p % 64].T
    P[:, off["fox_gk"]:off["fox_gk"] + 2] = inp["fox_k_norm_g"][:, p % 64].T
    fb = np.zeros((128, 2), np.float32)
    for g in range(4):
        fb[g * 32:g * 32 + 8, :] = inp["fox_forget_b"].T
    P[:, off["fox_b"]:off["fox_b"] + 2] = fb
    P[:, off["diff_gq"]:off["diff_gq"] + 2] = inp["diff_q_norm_g"][:, p % 64].T
    P[:, off["diff_gk"]:off["diff_gk"] + 2] = inp["diff_k_norm_g"][:, p % 64].T
    P[:, off["subln"]:off["subln"] + 2] = inp["diff_subln_g"].T
    for nm, key in (("lq1", "diff_lambda_q1"), ("lk1", "diff_lambda_k1"), ("lq2", "diff_lambda_q2"), ("lk2", "diff_lambda_k2")):
        P[:, off[nm]:off[nm] + 128] = np.broadcast_to(inp[key].reshape(1, 128), (128, 128))
    P[:, off["relb"]:off["relb"] + 256] = np.broadcast_to(inp["rel_bias"].reshape(1, 256), (128, 256))
    m1 = np.zeros(128, np.float32)
    m2 = np.zeros(128, np.float32)
    m1[0:8] = -1.0
    m1[64:72] = 1.0
    m2[32:40] = -1.0
    m2[96:104] = 1.0
    P[:, off["m1"]] = m1
    P[:, off["m2"]] = m2
    for h in range(8):
        a = np.zeros(128, np.float32)
        b = np.zeros(128, np.float32)
        a[h] = 1.0
        a[32 + h] = 1.0
        b[64 + h] = 1.0
        b[96 + h] = 1.0
        P[:, off["selA"] + h] = a
        P[:, off["selB"] + h] = b
    s = np.arange(128)[:, None]
    t = np.arange(128)[None, :]
    d0 = t - s
    d1 = t - s + 128
    bk = np.concatenate([t5_bucket_np(np.maximum(d0, 0)), t5_bucket_np(d1)], axis=1).astype(np.float32)
    P[:, off["bk"]:off["bk"] + 256] = bk
    mask0 = np.zeros((128, 256), np.float32)
    mask0[:, 0:128] = np.where(s > t, NEG, 0.0)
    P[:, off["mask0"]:off["mask0"] + 256] = mask0
    return P


def build_cmat():
    j = np.arange(128)[:, None]
    s = np.arange(128)[None, :]
    ident = (j == s).astype(np.float32)
    negtri = np.where(j >= s, -1.0, 0.0).astype(np.float32)
    negones = -np.ones((128, 128), np.float32)
    ones = np.ones((128, 128), np.float32)
    blk = ((j // 64) == (s // 64)).astype(np.float32)
    maskS = np.where(j >= s, NEG, 0.0).astype(np.float32)
    maskI = np.where(j > s, NEG, 0.0).astype(np.float32)
    return np.concatenate([ident, negtri, negones, ones, blk, maskS, maskI], axis=1)


C_ID, C_NTRI, C_NONES, C_ONES, C_BLK, C_MS, C_MI = range(7)


def build_program(S=2048, NSEQ=2, LAYERS=4):
    assert S % 512 == 0
    NTC = S // 512
    NKB = S // 128
    nc = bass.Bass("TRN2", target_bir_lowering=False)
    poff, NP = param_layout()

    def din(name, shape):
        return nc.dram_tensor(name, list(shape), F32, kind="ExternalInput").ap()

    xT_d = din("xT", (NSEQ, 8, 128, S))
    params_d = din("params", (128, NP))
    cmat_d = din("cmat", (128, 7 * 128))
    win_e_d = din("win_e", (2, 24, 128, 1024))
    wfg_d = din("wfg", (2, 128, 1024))
    wout_e_d = din("wout_e", (2, 1024, 1024))
    win_d_d = din("win_d", (2, 24, 128, 1024))
    wout_d_d = din("wout_d", (2, 1024, 1024))
    wg_d = din("wg", (4, NJ, 128, 1024))
    wu_d = din("wu", (4, NJ, 128, 1024))
    wd_d = din("wd", (4, 8, 128, DFF))
    yT_d = nc.dram_tensor("yT", [NSEQ, 8, 128, S], F32, kind="ExternalOutput").ap()

    st = ExitStack()
    with st:
        Sc = Sched(nc, st)
        add = Sc.add

        def sb(name, n, dt):
            return st.enter_context(nc.sbuf_tensor("sb_" + name, [128, n], dt))

        xT = sb("xT", 8 * S, F32)
        hT = sb("hT", 8 * S, BF16)
        prm = sb("prm", NP, F32)
        cm = sb("cm", 7 * 128, BF16)
        rs2 = sb("rs2", 512, F32)
        sq2 = sb("sq2", 512, BF16)
        biasT = sb("biasT", 8 * 256, BF16)
        small = sb("small", 64, F32)
        C_EPS, C_ONE, C_NEGB0, C_NEGB1, C_GQ8F0, C_GQ8F1, C_GQ8D0, C_GQ8D1 = range(8)
        C_NLAM = 8
        C_GS = 10
        C_TMP = 12
        A16N = 38912
        A32N = 4864
        a16 = sb("a16", A16N, BF16)
        a32 = sb("a32", A32N, F32)
        ps = st.enter_context(nc.psum_tensor("ps", [128, 4096], F32))

        def bank(b):
            return ps[:, b * 512:(b + 1) * 512]

        o = 0

        def carve(n):
            nonlocal o
            r = o
            o += n
            return r

        OTC = [carve(S), carve(S)]
        QT = [carve(S), carve(S)]
        KT = [carve(S), carve(S)]
        VV = [carve(NKB * 128), carve(NKB * 128)]
        QH = [carve(S), carve(S)]
        KH = [carve(S), carve(S)]
        CALL = carve(S)
        KP2 = [QH[1], KH[1]]
        WSLOT = [carve(1024) for _ in range(5)]
        LP = [carve(512), carve(512)]
        LSUM = [carve(512), carve(512)]
        WT = [carve(512) for _ in range(4)]
        SQ = [carve(512), carve(512)]
        HI = carve(512)
        LO = carve(512)
        TMPB = carve(512)
        ATT_END = o
        assert ATT_END <= A16N, ATT_END
        o = 0
        GT = carve(NJ * 1024)
        FW = [carve(1024) for _ in range(6)]
        WD = [carve(DFF), carve(DFF)]
        SQF = [carve(512), carve(512)]
        assert o <= A16N, o
        o = 0
        EB = [carve(512), carve(512)]
        RS = [carve(512), carve(512)]
        LG = carve(512)
        CP = [carve(512), carve(512)]
        T1 = carve(512)
        OD = carve(256)
        RD = carve(512)
        assert o <= A32N, o
        STM = [EB[0], EB[1]]

        def A(off, n):
            return a16[:, off:off + n]

        def A3(off, n):
            return a32[:, off:off + n]

        def cmat(i):
            return cm[:, i * 128:(i + 1) * 128]

        def pcol(name, j=0):
            c = poff[name] + j
            return prm[:, c:c + 1]

        def scol(j):
            return small[:, j:j + 1]

        ARENA_ATT = ["att16"]
        ARENA_FFN = ["ffn16"]

        def mm(out, lhsT, rhs, start, stop, reads, writes, skip=False):
            return add("pe", lambda e: e.matmul(out, lhsT=lhsT, rhs=rhs, start=start, stop=stop,
                                                skip_group_check=skip), reads, writes)

        def act(out, in_, func, reads, writes, bias=None, scale=None):
            kw = {}
            if bias is not None:
                kw["bias"] = bias
            if scale is not None:
                kw["scale"] = scale
            return add("act", lambda e: e.activation(out=out, in_=in_, func=func, **kw), reads, writes)

        def dma(eng, out, in_, reads, writes):
            return add(eng, lambda e: e.dma_start(out=out, in_=in_), reads, writes, dma=True)

        dma("sp", prm[:, :], params_d, [], ["prm"])
        dma("pool", cm[:, :], cmat_d, [], ["cm"])
        add("dve", lambda e: e.memset(small[:, C_EPS:C_EPS + 1], EPS), [], ["small"])
        add("dve", lambda e: e.memset(small[:, C_ONE:C_ONE + 1], 1.0), [], ["small"])
        for e_ in range(2):
            add("dve", lambda e, e_=e_: e.tensor_scalar(out=small[:, C_NEGB0 + e_:C_NEGB0 + e_ + 1], in0=pcol("fox_b", e_),
                                                        scalar1=-1.0, scalar2=None, op0=ALU.mult),
                ["prm", "small"], ["small"])
            add("dve", lambda e, e_=e_: e.tensor_scalar(out=small[:, C_GQ8F0 + e_:C_GQ8F0 + e_ + 1], in0=pcol("fox_gq", e_),
                                                        scalar1=0.125, scalar2=None, op0=ALU.mult),
                ["prm", "small"], ["small"])
            add("dve", lambda e, e_=e_: e.tensor_scalar(out=small[:, C_GQ8D0 + e_:C_GQ8D0 + e_ + 1], in0=pcol("diff_gq", e_),
                                                        scalar1=0.125, scalar2=None, op0=ALU.mult),
                ["prm", "small"], ["small"])

        gen_state = {"banks": [7], "i": 0}

        def gbank():
            b = gen_state["banks"][gen_state["i"] % len(gen_state["banks"])]
            gen_state["i"] += 1
            return b

        wslot_i = [0]

        def load_w(src_ap, n=1024, slots=WSLOT, ctr=wslot_i, tag="ws"):
            i = ctr[0] % len(slots)
            ctr[0] += 1
            key = (tag, i)
            dma("pool", A(slots[i], n), src_ap, [], [key])
            return slots[i], key

        def norm(gname, l, sqbufs):
            for tc in range(NTC):
                nb = 7
                for c in range(8):
                    sq = sqbufs[c % 2]
                    xs = xT[:, c * S + tc * 512: c * S + tc * 512 + 512]
                    if c % 2 == 0:
                        add("pool", lambda e, sq=sq, xs=xs: e.tensor_tensor(out=A(sq, 512), in0=xs, in1=xs, op=ALU.mult),
                            [("xT", c, tc)], [("sq", sq)])
                    else:
                        act(A(sq, 512), xs, AF.Square, [("xT", c, tc)], [("sq", sq)])
                    mm(bank(nb), cmat(C_ONES), A(sq, 512), c == 0, c == 7, [("sq", sq), "cm"], [("ps", nb)])
                act(A3(RS[0], 512), bank(nb), AF.Ln, [("ps", nb), "small"], [("rs", 0)], bias=scol(C_EPS), scale=1.0 / D)
                act(A3(RS[1], 512), A3(RS[0], 512), AF.Exp, [("rs", 0)], [("rs", 1)], scale=-0.5)
                for c in range(8):
                    xs = xT[:, c * S + tc * 512: c * S + tc * 512 + 512]
                    hs = hT[:, c * S + tc * 512: c * S + tc * 512 + 512]
                    gc = pcol(gname, l * 8 + c)
                    add("dve", lambda e, xs=xs, hs=hs, gc=gc: e.scalar_tensor_tensor(
                        out=hs, in0=xs, scalar=gc, in1=A3(RS[1], 512), op0=ALU.mult, op1=ALU.mult),
                        [("xT", c, tc), ("rs", 1), "prm"], [("hT", c, tc)])
                yield "tc"

        def adv(gen):
            for v in gen:
                if v == "tc":
                    return

        def proj_feat(wkey, woff, dst_off, mode, gcol=None, dst2_off=None):
            for tc in range(NTC):
                b = gbank()
                for kc in range(8):
                    mm(bank(b), A(woff, 1024)[:, kc * 128:(kc + 1) * 128], hT[:, kc * S + tc * 512: kc * S + tc * 512 + 512],
                       kc == 0, kc == 7, [wkey, ("hT", kc, tc)], [("ps", b)])
                yield
                if dst2_off is None:
                    parts = [(slice(0, 128), dst_off)]
                else:
                    parts = [(slice(0, 64), dst_off), (slice(64, 128), dst2_off)]
                if mode == "qknorm":
                    sq = SQ[tc % 2]
                    act(A(sq, 512), bank(b), AF.Square, [("ps", b)], [("sq", sq)])
                    b2 = gbank()
                    mm(bank(b2), cmat(C_BLK), A(sq, 512), True, True, [("sq", sq), "cm"], [("ps", b2)])
                    yield
                    act(A3(RS[0], 512), bank(b2), AF.Ln, [("ps", b2), "small"], [("rs", 0)], bias=scol(C_EPS), scale=1.0 / HD)
                    act(A3(RS[1], 512), A3(RS[0], 512), AF.Exp, [("rs", 0)], [("rs", 1)], scale=-0.5)
                    yield
                for (rs_, doff) in parts:
                    dst = A(doff, S)[rs_, tc * 512:(tc + 1) * 512]
                    src = bank(b)[rs_, :]
                    dkey = ("buf", doff, tc)
                    if mode == "scale8":
                        add("dve", lambda e, dst=dst, src=src: e.tensor_scalar(out=dst, in0=src, scalar1=0.125, scalar2=None,
                                                                               op0=ALU.mult), [("ps", b)], [dkey])
                    elif mode == "copy":
                        add("dve", lambda e, dst=dst, src=src: e.tensor_copy(out=dst, in_=src), [("ps", b)], [dkey])
                    else:
                        add("dve", lambda e, dst=dst, src=src, rs_=rs_, gcol=gcol: e.scalar_tensor_tensor(
                            out=dst, in0=src, scalar=gcol[rs_, :], in1=A3(RS[1], 512)[rs_, :], op0=ALU.mult, op1=ALU.mult),
                            [("ps", b), ("rs", 1), "prm", "small"], [dkey])
                yield "tc"

        def proj_v(wkey, woff, dst_off):
            for g in range(NKB // 4):
                b = gbank()
                for i in range(4):
                    kb = 4 * g + i
                    for kc in range(8):
                        mm(bank(b)[:, i * 128:(i + 1) * 128], hT[:, kc * S + kb * 128: kc * S + kb * 128 + 128],
                           A(woff, 1024)[:, kc * 128:(kc + 1) * 128], kc == 0, kc == 7,
                           [wkey, ("hT", kc, kb // 4)], [("ps", b)], skip=True)
                    if i % 2 == 1:
                        yield
                dst = A(dst_off, NKB * 128)[:, g * 512:(g + 1) * 512]
                add("dve", lambda e, dst=dst, b=b: e.tensor_copy(out=dst, in_=bank(b)), [("ps", b)], [("buf", dst_off, g)])
                yield "tc"

        def out_proj_partial(w_rows_ap, otc_off):
            woff, wkey = load_w(w_rows_ap)
            for m in range(8):
                for tc in range(NTC):
                    b = gbank()
                    mm(bank(b), A(woff, 1024)[:, m * 128:(m + 1) * 128], A(otc_off, S)[:, tc * 512:(tc + 1) * 512],
                       True, True, [wkey, ("buf", otc_off, tc)], [("ps", b)])
                    xs = xT[:, m * S + tc * 512: m * S + tc * 512 + 512]
                    add("dve", lambda e, xs=xs, b=b: e.tensor_tensor(out=xs, in0=xs, in1=bank(b), op=ALU.add),
                        [("ps", b), ("xT", m, tc)], [("xT", m, tc)])
                    if tc % 2 == 1:
                        yield

        bgq = []

        def pump(n=1):
            for _ in range(n):
                while bgq:
                    try:
                        next(bgq[0])
                        break
                    except StopIteration:
                        bgq.pop(0)

        def drain():
            while bgq:
                pump()

        def run(gen):
            for _ in gen:
                pass

        def vkeys(voff):
            return [("buf", voff, g) for g in range(NKB // 4)]

        def tkeys(off):
            return [("buf", off, tc) for tc in range(NTC)]

        ones32 = A3(LG, 512)[:, 0:128]

        def wsum_bufs(qc):
            if qc % 2 == 0:
                return (EB[0], ("eb", 0)), (EB[1], ("eb", 1))
            return (CP[0], ("cp", 0)), (CP[1], ("cp", 1))

        def den_acc(n, qc, q0, w, wkey, two_maps=False):
            boff, bkey = wsum_bufs(qc)[n % 2]
            eng = "dve"
            full = A3(boff, 512)
            if two_maps:
                v = full.rearrange("p (m q) -> p m q", m=2)
                dstv = v[:, :, q0:256]
                zerov = v[:, :, 0:q0] if q0 > 0 else None
            else:
                dstv = full[:, q0:512]
                zerov = full[:, 0:q0] if q0 > 0 else None
            if n < 2:
                if zerov is not None:
                    add(eng, lambda e: e.memset(zerov, 0.0), [], [bkey])
                add(eng, lambda e: e.tensor_copy(out=dstv, in_=w), [wkey], [bkey])
            else:
                add(eng, lambda e: e.tensor_tensor(out=dstv, in0=dstv, in1=w, op=ALU.add), [wkey, bkey], [bkey])

        def sb_head(qoff, koff, voff, otc_off, pofs, ctr):
            pairs = []
            for qc in range(NTC):
                kbs = list(range(min(4 * qc + 3, NKB - 1), -1, -1))
                for n, kb in enumerate(kbs):
                    pairs.append(dict(qc=qc, kb=kb, q0=max(0, 128 * (kb - 4 * qc)), first=(n == 0), last=(kb == 0), n=n))
            NZ = 3
            qk_r = tkeys(qoff) + tkeys(koff)

            def zb(i):
                return i % NZ

            def stA(i):
                p = pairs[i]
                q0 = p["q0"]
                z = bank(zb(i))
                diag = p["kb"] >= 4 * p["qc"]
                mm(z[:, q0:512], A(koff, S)[:, p["kb"] * 128:(p["kb"] + 1) * 128],
                   A(qoff, S)[:, p["qc"] * 512 + q0:(p["qc"] + 1) * 512], True, not diag,
                   qk_r, [("ps", zb(i))], skip=True)
                if diag:
                    mm(z[:, q0:q0 + 128], cmat(C_ID), cmat(C_MS), False, True, ["cm"], [("ps", zb(i))], skip=True)

            def stB(i):
                p = pairs[i]
                q0 = p["q0"]
                act(A3(EB[i % 2], 512)[:, q0:512], bank(zb(i))[:, q0:512], AF.Exp, [("ps", zb(i))], [("eb", i % 2)])

            def stC(i):
                p = pairs[i]
                q0 = p["q0"]
                act(A(LP[i % 2], 512)[:, q0:512], A3(EB[i % 2], 512)[:, q0:512], AF.Ln, [("eb", i % 2), "small"],
                    [("lp", i % 2)], bias=scol(C_ONE))

            def stD(i):
                p = pairs[i]
                q0 = p["q0"]
                z = bank(zb(i))
                if p["first"]:
                    for k in range(2):
                        add("pool", lambda e, k=k: e.memset(A(LSUM[k], 512), 0.0), [], [("lsum", k)])
                cur = p["n"] % 2
                prv = 1 - cur
                mm(z[:, q0:512], cmat(C_NTRI), A(LP[i % 2], 512)[:, q0:512], False, p["first"],
                   [("lp", i % 2), "cm", ("ps", zb(i))], [("ps", zb(i))], skip=True)
                if not p["first"]:
                    mm(z[:, q0:512], cmat(C_NONES), A(LSUM[prv], 512)[:, q0:512], False, True,
                       [("lsum", prv), "cm", ("ps", zb(i))], [("ps", zb(i))], skip=True)
                if not p["last"]:
                    lpi = LP[i % 2]
                    if p["first"]:
                        add("pool", lambda e, lpi=lpi, cur=cur, q0=q0: e.tensor_copy(out=A(LSUM[cur], 512)[:, q0:512],
                                                                                     in_=A(lpi, 512)[:, q0:512]),
                            [("lp", i % 2)], [("lsum", cur)])
                    else:
                        add("pool", lambda e, lpi=lpi, cur=cur, prv=prv, q0=q0: e.tensor_tensor(
                            out=A(LSUM[cur], 512)[:, q0:512], in0=A(LSUM[prv], 512)[:, q0:512], in1=A(lpi, 512)[:, q0:512],
                            op=ALU.add), [("lp", i % 2), ("lsum", prv)], [("lsum", cur)])

            def stE(i):
                p = pairs[i]
                q0 = p["q0"]
                act(A(WT[i % 4], 512)[:, q0:512], bank(zb(i))[:, q0:512], AF.Exp, [("ps", zb(i))], [("wt", i % 4)])

            def stF(i):
                p = pairs[i]
                q0 = p["q0"]
                if p["first"]:
                    ctr[0] += 1
                ob = 3 + ctr[0] % 2
                mm(bank(ob)[pofs:pofs + 64, q0:512], A(voff, NKB * 128)[:, p["kb"] * 128 + pofs: p["kb"] * 128 + pofs + 64],
                   A(WT[i % 4], 512)[:, q0:512], p["first"], p["last"], [("wt", i % 4)] + vkeys(voff), [("ps", ob)], skip=True)
                if p["last"]:
                    dst = A(otc_off, S)[pofs:pofs + 64, p["qc"] * 512:(p["qc"] + 1) * 512]
                    add("dve", lambda e, dst=dst, ob=ob: e.tensor_copy(out=dst, in_=bank(ob)[pofs:pofs + 64, :]),
                        [("ps", ob)], [("buf", otc_off, p["qc"])])

            n = len(pairs)
            for t in range(-1, n + 2):
                if 0 <= t < n:
                    stB(t)
                if 0 <= t - 2 < n:
                    stE(t - 2)
                if 0 <= t + 1 < n:
                    stA(t + 1)
                if 0 <= t < n:
                    stC(t)
                    stD(t)
                if 0 <= t - 2 < n:
                    stF(t - 2)
                pump()

        def fox_head(qhoff, khoff, qoff, koff, voff, otc_off, pofs, ctr):
            pairs = []
            for qc in range(NTC):
                kbs = list(range(min(4 * qc + 3, NKB - 1), -1, -1))
                for n, kb in enumerate(kbs):
                    pairs.append(dict(qc=qc, kb=kb, q0=max(0, 128 * (kb - 4 * qc)), first=(n == 0), last=(kb == 0), n=n))
            ZB = [0, 1, 6]
            qk_r = tkeys(qoff) + tkeys(koff)

            def stA(i):
                p = pairs[i]
                q0 = p["q0"]
                z = bank(ZB[i % 3])
                diag = p["kb"] >= 4 * p["qc"]
                ks = slice(p["kb"] * 128, (p["kb"] + 1) * 128)
                qs = slice(p["qc"] * 512 + q0, (p["qc"] + 1) * 512)
                mm(z[:, q0:512], A(koff, S)[:, ks], A(qoff, S)[:, qs], True, False,
                   qk_r, [("ps", ZB[i % 3])], skip=True)
                mm(z[:, q0:512], A(khoff, S)[:, ks], A(qhoff, S)[:, qs], False, not diag,
                   [("aug", khoff), ("aug", qhoff)], [("ps", ZB[i % 3])], skip=True)
                if diag:
                    mm(z[:, q0:q0 + 128], cmat(C_ID), cmat(C_MI), False, True, ["cm"], [("ps", ZB[i % 3])], skip=True)

            def stB(i):
                p = pairs[i]
                q0 = p["q0"]
                act(A(WT[i % 4], 512)[:, q0:512], bank(ZB[i % 3])[:, q0:512], AF.Exp, [("ps", ZB[i % 3])], [("wt", i % 4)])

            def stF(i):
                p = pairs[i]
                q0 = p["q0"]
                if p["first"]:
                    ctr[0] += 1
                ob = 3 + ctr[0] % 2
                db = 5
                w = A(WT[i % 4], 512)[:, q0:512]
                mm(bank(ob)[pofs:pofs + 64, q0:512], A(voff, NKB * 128)[:, p["kb"] * 128 + pofs: p["kb"] * 128 + pofs + 64],
                   w, p["first"], p["last"], [("wt", i % 4)] + vkeys(voff), [("ps", ob)], skip=True)
                den_acc(p["n"], p["qc"], q0, w, ("wt", i % 4))
                if p["last"]:
                    (b0, k0), (b1, k1) = wsum_bufs(p["qc"])
                    mm(bank(db), ones32, A3(b0, 512), True, False, [k0, "lg"], [("ps", db)], skip=True)
                    mm(bank(db), ones32, A3(b1, 512), False, True, [k1, "lg"], [("ps", db)], skip=True)
                    rd = A3(RD, 512)[pofs:pofs + 64, :]
                    act(rd, bank(db)[pofs:pofs + 64, :], AF.Ln, [("ps", db)], ["rd"])
                    act(rd, rd, AF.Exp, ["rd"], ["rd"], scale=-1.0)
                    dst = A(otc_off, S)[pofs:pofs + 64, p["qc"] * 512:(p["qc"] + 1) * 512]
                    add("dve", lambda e, dst=dst, ob=ob, rd=rd: e.tensor_tensor(out=dst, in0=bank(ob)[pofs:pofs + 64, :],
                                                                               in1=rd, op=ALU.mult),
                        [("ps", ob), "rd"], [("buf", otc_off, p["qc"])])

            n = len(pairs)
            for t in range(-2, n):
                if 0 <= t + 2 < n:
                    stA(t + 2)
                if 0 <= t + 1 < n:
                    stB(t + 1)
                if 0 <= t < n:
                    stF(t)
                pump()

        def diff_head(h, qoff, koff, koff2, voff, otc_off, o_idx, ctr):
            NQ = S // 256
            pairs = []
            for qc in range(NQ):
                kbs = list(range(min(2 * qc + 1, NKB - 1), -1, -1))
                for n, kb in enumerate(kbs):
                    pairs.append(dict(qc=qc, kb=kb, q0=(128 if kb == 2 * qc + 1 else 0), first=(n == 0), last=(kb == 0), n=n))
            ZB = [0, 1, 6]
            qk_r = tkeys(qoff) + tkeys(koff) + tkeys(koff2)
            bdiag = biasT[:, h * 256: h * 256 + 128]
            boff = biasT[:, h * 256 + 128: h * 256 + 256]

            def v2(ap, q0):
                return ap.rearrange("p (m q) -> p m q", m=2)[:, :, q0:256]

            def stA(i):
                p = pairs[i]
                q0 = p["q0"]
                z = bank(ZB[i % 3])
                ks = slice(p["kb"] * 128, (p["kb"] + 1) * 128)
                qs = slice(p["qc"] * 256 + q0, (p["qc"] + 1) * 256)
                d = p["kb"] - 2 * p["qc"]
                near = d >= -1
                mm(z[:, q0:256], A(koff, S)[:, ks], A(qoff, S)[:, qs], True, False, qk_r, [("ps", ZB[i % 3])], skip=True)
                mm(z[:, 256 + q0:512], A(koff2, S)[:, ks], A(qoff, S)[:, qs], False, not near, qk_r,
                   [("ps", ZB[i % 3])], skip=True)
                if near:
                    blocks = []
                    if d == 1:
                        blocks = [(128, bdiag)]
                    elif d == 0:
                        blocks = [(0, bdiag), (128, boff)]
                    else:
                        blocks = [(0, boff)]
                    k = 0
                    for m_ in range(2):
                        for (c0, bt) in blocks:
                            k += 1
                            mm(z[:, m_ * 256 + c0: m_ * 256 + c0 + 128], cmat(C_ID), bt, False, k == 2 * len(blocks),
                               ["cm", "biasT"], [("ps", ZB[i % 3])], skip=True)

            def stB(i):
                p = pairs[i]
                q0 = p["q0"]
                act(v2(A(WT[i % 4], 512), q0), v2(bank(ZB[i % 3]), q0), AF.Exp, [("ps", ZB[i % 3])], [("wt", i % 4)])

            def stF(i):
                p = pairs[i]
                q0 = p["q0"]
                qc = p["qc"]
                if p["first"]:
                    ctr[0] += 1
                ob = 3 + ctr[0] % 2
                db = 5
                vk = A(voff, NKB * 128)[:, p["kb"] * 128:(p["kb"] + 1) * 128]
                if q0 == 0:
                    segs = [(0, 512)]
                else:
                    segs = [(q0, 256), (256 + q0, 512)]
                for si, (c0, c1) in enumerate(segs):
                    w = A(WT[i % 4], 512)[:, c0:c1]
                    mm(bank(ob)[:, c0:c1], vk, w, p["first"] and si == 0, p["last"] and si == len(segs) - 1,
                       [("wt", i % 4)] + vkeys(voff), [("ps", ob)], skip=True)
                den_acc(p["n"], qc, q0, v2(A(WT[i % 4], 512), q0), ("wt", i % 4), two_maps=True)
                if p["last"]:
                    exhaust_tails()
                    tailq.append(chunk_tail(ob, db, qc))

            tailq = []

            def pump_tail():
                if tailq:
                    try:
                        next(tailq[0])
                    except StopIteration:
                        tailq.pop(0)

            def exhaust_tails():
                while tailq:
                    pump_tail()

            def chunk_tail(ob, db, qc):
                (b0, k0), (b1, k1) = wsum_bufs(qc)
                mm(bank(db), ones32, A3(b0, 512), True, False, [k0, "lg"], [("ps", db)], skip=True)
                mm(bank(db), ones32, A3(b1, 512), False, True, [k1, "lg"], [("ps", db)], skip=True)
                yield
                act(A3(RD, 512), bank(db), AF.Ln, [("ps", db)], ["rd"])
                act(A3(RD, 512), A3(RD, 512), AF.Exp, ["rd"], ["rd"], scale=-1.0)
                yield
                add("dve", lambda e, ob=ob: e.tensor_tensor(out=A3(T1, 512), in0=bank(ob), in1=A3(RD, 512), op=ALU.mult),
                    [("ps", ob), "rd"], ["t1"])
                add("dve", lambda e: e.scalar_tensor_tensor(out=A3(OD, 256), in0=A3(T1, 512)[:, 256:512],
                                                            scalar=scol(C_NLAM + o_idx), in1=A3(T1, 512)[:, 0:256],
                                                            op0=ALU.mult, op1=ALU.add), ["t1", "small"], ["od"])
                yield
                sqa = sq2[:, (qc % 2) * 256:(qc % 2) * 256 + 256]
                sqk = ("sq2", qc % 2)
                add("pool", lambda e, sqa=sqa: e.tensor_tensor(out=sqa, in0=A3(OD, 256), in1=A3(OD, 256),
                                                               op=ALU.mult), ["od"], [sqk])
                yield
                nb = db
                mm(bank(nb)[:, 0:256], cmat(C_ONES), sqa, True, True, [sqk, "cm"], [("ps", nb)])
                yield
                act(rs2[:, 0:256], bank(nb)[:, 0:256], AF.Ln, [("ps", nb), "small"], [("rs2", 0)],
                    bias=scol(C_EPS), scale=1.0 / 128)
                act(rs2[:, 256:512], rs2[:, 0:256], AF.Exp, [("rs2", 0)], [("rs2", 1)], scale=-0.5)
                yield
                dst = A(otc_off, S)[:, qc * 256:(qc + 1) * 256]
                add("dve", lambda e, dst=dst: e.scalar_tensor_tensor(out=dst, in0=A3(OD, 256), scalar=scol(C_GS + o_idx),
                                                                     in1=rs2[:, 256:512], op0=ALU.mult, op1=ALU.mult),
                    ["od", ("rs2", 1), "small"], [("buf", otc_off, qc // 2)])

            n = len(pairs)
            for t in range(-2, n):
                if 0 <= t + 2 < n:
                    stA(t + 2)
                if 0 <= t + 1 < n:
                    stB(t + 1)
                if 0 <= t < n:
                    stF(t)
                pump_tail()
                pump()
            exhaust_tails()

        def even_layer(e_, ng):
            octr = [0]
            woff, wkey = load_w(wfg_d[e_])
            gen_state["banks"] = [5, 6, 7]

            def foxc(tc):
                b = gbank()
                for kc in range(8):
                    mm(bank(b), A(woff, 1024)[:, kc * 128:(kc + 1) * 128], hT[:, kc * S + tc * 512: kc * S + tc * 512 + 512],
                       kc == 0, kc == 7, [wkey, ("hT", kc, tc)], [("ps", b)])
                act(A3(EB[0], 512), bank(b), AF.Exp, [("ps", b), "small"], [("eb", 0)], bias=scol(C_NEGB0 + e_), scale=-1.0)
                act(A3(LG, 512), A3(EB[0], 512), AF.Ln, [("eb", 0), "small"], ["lg"], bias=scol(C_ONE))
                cpb = CP[tc % 2]
                init = 0.0 if tc == 0 else A3(CP[(tc - 1) % 2], 512)[:, 511:512]
                add("dve", lambda e, cpb=cpb, init=init: e.tensor_tensor_scan(
                    out=A3(cpb, 512), data0=small[:, C_ONE:C_ONE + 1].to_broadcast([128, 512]), data1=A3(LG, 512),
                    initial=init, op0=ALU.mult, op1=ALU.add),
                    ["lg", "small", ("cp", (tc - 1) % 2)], [("cp", tc % 2)])
                add("dve", lambda e, cpb=cpb: e.tensor_copy(out=A(HI, 512), in_=A3(cpb, 512)), [("cp", tc % 2)], ["hi"])
                add("dve", lambda e, cpb=cpb: e.tensor_tensor(out=A(LO, 512), in0=A3(cpb, 512), in1=A(HI, 512), op=ALU.subtract),
                    [("cp", tc % 2), "hi"], ["lo"])
                add("dve", lambda e: e.tensor_scalar(out=A(TMPB, 512), in0=A(HI, 512), scalar1=pcol("m1"), scalar2=None,
                                                     op0=ALU.mult), ["hi", "prm"], ["tmpb"])
                add("dve", lambda e, tc=tc: e.scalar_tensor_tensor(out=A(CALL, S)[:, tc * 512:(tc + 1) * 512], in0=A(LO, 512),
                                                                  scalar=pcol("m2"), in1=A(TMPB, 512), op0=ALU.mult, op1=ALU.add),
                    ["lo", "tmpb", "prm"], [("call", tc)])

            def pair_inproj(pi):
                fox = pi >= 4
                c = pi % 4
                buf = pi % 2
                jq, jk, jv = (12 + c, 16 + c, 20 + c) if fox else (c, 4 + c, 8 + c)
                wq, kq = load_w(win_e_d[e_, jq])
                wk, kk = load_w(win_e_d[e_, jk])
                wv, kv = load_w(win_e_d[e_, jv])
                if fox:
                    yield from proj_feat(kq, wq, QT[buf], "qknorm", scol(C_GQ8F0 + e_))
                    yield from proj_feat(kk, wk, KT[buf], "qknorm", pcol("fox_gk", e_), dst2_off=KP2[buf])
                else:
                    yield from proj_feat(kq, wq, QT[buf], "scale8")
                    yield from proj_feat(kk, wk, KT[buf], "copy", dst2_off=KP2[buf])
                yield from proj_v(kv, wv, VV[buf])

            gen_state["banks"] = [5, 6, 7]
            wq0, kq0 = load_w(win_e_d[e_, 0])
            wk0, kk0 = load_w(win_e_d[e_, 4])
            wv0, kv0 = load_w(win_e_d[e_, 8])
            g0 = [proj_feat(kq0, wq0, QT[0], "scale8"), proj_feat(kk0, wk0, KT[0], "copy", dst2_off=KP2[0]),
                  proj_v(kv0, wv0, VV[0])]
            for tc in range(NTC):
                adv(ng)
                foxc(tc)
                for g_ in g0:
                    adv(g_)
            for g_ in g0:
                run(g_)
            run(ng)
            add("dve", lambda e: e.memset(ones32, 1.0), [], ["lg"])
            for pi in range(8):
                fox = pi >= 4
                c = pi % 4
                buf = pi % 2
                gen_state["banks"] = [7, 2] if fox else [5, 6, 7]
                if pi >= 1:
                    bgq.append(out_proj_partial(wout_e_d[e_, (pi - 1) * 128:pi * 128, :], OTC[1 - buf]))
                if pi + 1 < 8:
                    bgq.append(pair_inproj(pi + 1))
                for hh in range(2):
                    pofs = 64 * hh
                    if fox:
                        h = 2 * c + hh
                        calls = [("call", tc) for tc in range(NTC)]
                        add("dve", lambda e, h=h: e.tensor_scalar(out=A(QH[0], S), in0=A(CALL, S), scalar1=pcol("selA", h),
                                                                  scalar2=pcol("selB", h), op0=ALU.mult, op1=ALU.add),
                            calls + ["prm"], [("aug", QH[0])])
                        add("dve", lambda e, h=h: e.tensor_scalar(out=A(KH[0], S), in0=A(CALL, S), scalar1=pcol("selB", h),
                                                                  scalar2=pcol("selA", h), op0=ALU.mult, op1=ALU.add),
                            calls + ["prm"], [("aug", KH[0])])
                        fox_head(QH[0], KH[0], QT[buf], (KT[buf], KP2[buf])[hh], VV[buf], OTC[buf], pofs, octr)
                    else:
                        sb_head(QT[buf], (KT[buf], KP2[buf])[hh], VV[buf], OTC[buf], pofs, octr)
                drain()
            run(out_proj_partial(wout_e_d[e_, 7 * 128:8 * 128, :], OTC[1]))

        def odd_layer(o_, l, first_time, ng):
            octr = [0]
            lam_init = 0.8 - 0.6 * math.exp(-0.3 * l)
            if first_time[0]:
                first_time[0] = False
                tb = A3(T1, 512)[:, 0:256]
                add("dve", lambda e: e.tensor_tensor(
                    out=tb.rearrange("p (b h) -> p b h", h=8),
                    in0=prm[:, poff["relb"]:poff["relb"] + 256].rearrange("p (b h) -> p b h", h=8),
                    in1=prm[:, poff["relb"] + 248:poff["relb"] + 256].unsqueeze(1).to_broadcast([128, 32, 8]),
                    op=ALU.subtract), ["prm"], ["t1"])
                acc = A3(RD, 512)[:, 0:256]
                tmp = A3(RD, 512)[:, 256:512]
                bkt = prm[:, poff["bk"]:poff["bk"] + 256]
                for h in range(8):
                    add("dve", lambda e: e.tensor_copy(out=acc, in_=prm[:, poff["mask0"]:poff["mask0"] + 256]), ["prm"], ["rd"])
                    for b_ in range(NBK):
                        col = A3(T1, 512)[:, b_ * 8 + h: b_ * 8 + h + 1]
                        add("dve", lambda e, col=col, b_=b_: e.tensor_scalar(out=tmp, in0=bkt, scalar1=float(b_), scalar2=col,
                                                                             op0=ALU.is_equal, op1=ALU.mult),
                            ["prm", "t1"], ["rdt"])
                        add("dve", lambda e: e.tensor_tensor(out=acc, in0=acc, in1=tmp, op=ALU.add), ["rd", "rdt"], ["rd"])
                    add("dve", lambda e, h=h: e.tensor_copy(out=biasT[:, h * 256:(h + 1) * 256], in_=acc), ["rd"], ["biasT"])
            for i_, (a_, b_) in enumerate((("lq1", "lk1"), ("lq2", "lk2"))):
                pa = prm[:, poff[a_] + o_ * 64: poff[a_] + o_ * 64 + 64]
                pb = prm[:, poff[b_] + o_ * 64: poff[b_] + o_ * 64 + 64]
                add("dve", lambda e, pa=pa, pb=pb: e.tensor_tensor(out=A3(OD, 256)[:, 0:64], in0=pa, in1=pb, op=ALU.mult),
                    ["prm"], ["od"])
                add("dve", lambda e, i_=i_: e.reduce_sum(out=small[:, C_TMP + i_:C_TMP + i_ + 1], in_=A3(OD, 256)[:, 0:64], axis=AX.X),
                    ["od"], ["small"])
                act(small[:, C_TMP + 2 + i_:C_TMP + 3 + i_], small[:, C_TMP + i_:C_TMP + i_ + 1], AF.Exp, ["small"], ["small"])
            add("dve", lambda e: e.tensor_tensor(out=small[:, C_TMP + 4:C_TMP + 5], in0=small[:, C_TMP + 3:C_TMP + 4],
                                                 in1=small[:, C_TMP + 2:C_TMP + 3], op=ALU.subtract), ["small"], ["small"])
            add("dve", lambda e: e.tensor_scalar(out=small[:, C_NLAM + o_:C_NLAM + o_ + 1], in0=small[:, C_TMP + 4:C_TMP + 5],
                                                 scalar1=-lam_init, scalar2=None, op0=ALU.add), ["small"], ["small"])
            add("dve", lambda e: e.tensor_scalar(out=small[:, C_GS + o_:C_GS + o_ + 1], in0=pcol("subln", o_),
                                                 scalar1=1.0 - lam_init, scalar2=None, op0=ALU.mult), ["small", "prm"], ["small"])
            gen_state["banks"] = [7, 2]
            add("dve", lambda e: e.memset(ones32, 1.0), [], ["lg"])

            def head_inproj(h):
                buf = h % 2
                wq, kq = load_w(win_d_d[o_, h])
                wk, kk = load_w(win_d_d[o_, 8 + h])
                wv, kv = load_w(win_d_d[o_, 16 + h])
                yield from proj_feat(kq, wq, QT[buf], "qknorm", scol(C_GQ8D0 + o_))
                yield from proj_feat(kk, wk, KT[buf], "qknorm", pcol("diff_gk", o_), dst2_off=KP2[buf])
                yield from proj_v(kv, wv, VV[buf])

            wq0, kq0 = load_w(win_d_d[o_, 0])
            wk0, kk0 = load_w(win_d_d[o_, 8])
            wv0, kv0 = load_w(win_d_d[o_, 16])
            g0 = [proj_feat(kq0, wq0, QT[0], "qknorm", scol(C_GQ8D0 + o_)),
                  proj_feat(kk0, wk0, KT[0], "qknorm", pcol("diff_gk", o_), dst2_off=KP2[0]),
                  proj_v(kv0, wv0, VV[0])]
            for tc in range(NTC):
                adv(ng)
                for g_ in g0:
                    adv(g_)
            for g_ in g0:
                run(g_)
            run(ng)
            for h in range(8):
                buf = h % 2
                if h >= 1:
                    bgq.append(out_proj_partial(wout_d_d[o_, (h - 1) * 128:h * 128, :], OTC[1 - buf]))
                if h + 1 < 8:
                    bgq.append(head_inproj(h + 1))
                diff_head(h, QT[buf], KT[buf], KP2[buf], VV[buf], OTC[buf], o_, octr)
                drain()
            run(out_proj_partial(wout_d_d[o_, 7 * 128:8 * 128, :], OTC[1]))

        fw_i = [0]
        wd_i = [0]

        def ffn(l, ng):
            NH = S // 1024 if S >= 1024 else 1
            TW = S // NH
            NT = TW // 512
            it = 0
            run(ng)
            for half in range(NH):
                for j in range(NJ):
                    wgo, kg = load_w(wg_d[l, j], slots=FW, ctr=fw_i, tag="fw")
                    wuo, ku = load_w(wu_d[l, j], slots=FW, ctr=fw_i, tag="fw")
                    for tl in range(NT):
                        tc = half * NT + tl
                        ba, bu = (0, 1) if it % 2 == 0 else (2, 3)
                        it += 1
                        for kc in range(8):
                            mm(bank(ba), A(wgo, 1024)[:, kc * 128:(kc + 1) * 128], hT[:, kc * S + tc * 512: kc * S + tc * 512 + 512],
                               kc == 0, kc == 7, [kg, ("hT", kc, tc)], [("ps", ba)])
                        for kc in range(8):
                            mm(bank(bu), A(wuo, 1024)[:, kc * 128:(kc + 1) * 128], hT[:, kc * S + tc * 512: kc * S + tc * 512 + 512],
                               kc == 0, kc == 7, [ku, ("hT", kc, tc)], [("ps", bu)])
                        stm = STM[it % 2]
                        act(A3(stm, 512), bank(ba), AF.Silu, [("ps", ba)], [("stm", stm)])
                        gdst = A(GT, NJ * 1024)[:, j * 1024 + tl * 512: j * 1024 + tl * 512 + 512]
                        add("dve", lambda e, gdst=gdst, stm=stm, bu=bu: e.tensor_tensor(out=gdst, in0=A3(stm, 512), in1=bank(bu),
                                                                                        op=ALU.mult),
                            [("stm", stm), ("ps", bu)], [("gT", j, tl)])
                for m in range(8):
                    wdo, kd = load_w(wd_d[l, m], n=DFF, slots=WD, ctr=wd_i, tag="wd")
                    for tl in range(NT):
                        tc = half * NT + tl
                        b = 4 + (m * NT + tl) % 2
                        for j in range(NJ):
                            mm(bank(b), A(wdo, DFF)[:, j * 128:(j + 1) * 128], A(GT, NJ * 1024)[:, j * 1024 + tl * 512: j * 1024 + tl * 512 + 512],
                               j == 0, j == NJ - 1, [kd, ("gT", j, tl)], [("ps", b)])
                        xs = xT[:, m * S + tc * 512: m * S + tc * 512 + 512]
                        add("dve", lambda e, xs=xs, b=b: e.tensor_tensor(out=xs, in0=xs, in1=bank(b), op=ALU.add),
                            [("ps", b), ("xT", m, tc)], [("xT", m, tc)])

        def att_keys():
            ks = []
            for off in OTC + QT + KT:
                ks += tkeys(off)
            for off in VV:
                ks += vkeys(off)
            for off in QH + KH:
                ks.append(("aug", off))
            for off in KP2:
                ks += tkeys(off)
            ks += [("call", tc) for tc in range(NTC)]
            ks += [("ws", i) for i in range(len(WSLOT))]
            ks += [("lp", 0), ("lp", 1), ("lsum", 0), ("lsum", 1), ("wt", 0), ("wt", 1), ("wt", 2), ("wt", 3)]
            ks += [("sq", SQ[0]), ("sq", SQ[1]), "hi", "lo", "tmpb"]
            return ks

        def ffn_keys():
            ks = [("gT", j, tl) for j in range(NJ) for tl in range(2)]
            ks += [("fw", i) for i in range(len(FW))] + [("wd", 0), ("wd", 1)]
            ks += [("sq", SQF[0]), ("sq", SQF[1])]
            return ks

        first_time = [True]
        outs = []
        for seq in range(NSEQ):
            for c in range(8):
                dma("sp", xT[:, c * S:(c + 1) * S], xT_d[seq, c], [], [("xT", c, tc) for tc in range(NTC)])
            for l in range(LAYERS):
                Sc.new_epoch()
                Sc.handoff(ffn_keys() + [("stm", STM[0]), ("stm", STM[1])], att_keys() + [("eb", 0), ("eb", 1)])
                for buf_ in range(2):
                    add("pool", lambda e, buf_=buf_: e.memset(A(KT[buf_], S)[64:128, :], 0.0), [], tkeys(KT[buf_]))
                    add("pool", lambda e, buf_=buf_: e.memset(A(KP2[buf_], S)[0:64, :], 0.0), [], tkeys(KP2[buf_]))
                ng = norm("g_attn", l, SQ)
                if l % 2 == 0:
                    even_layer(l // 2, ng)
                else:
                    odd_layer(l // 2, l, first_time, ng)
                Sc.handoff(att_keys() + [("eb", 0), ("eb", 1)], ffn_keys() + [("stm", STM[0]), ("stm", STM[1])])
                ffn(l, norm("g_ffn", l, SQF))
            for c in range(8):
                outs.append(dma("sp", yT_d[seq, c], xT[:, c * S:(c + 1) * S], [("xT", c, tc) for tc in range(NTC)], []))
        Sc.finalize(final_waits=outs)
        build_program.stats = dict(n_ops=len(Sc.ops), n_sems=Sc.n_sems, max_cnt=Sc.max_cnt,
                                   per_eng={e: len(v) for e, v in Sc.eng_ops.items()})
    return nc


def make_shared_inputs(inp):
    f = lambda a: np.ascontiguousarray(np.asarray(a, dtype=np.float32))
    ewin = f(inp["even_w_in"])
    win_e = np.stack([tile_w(ewin[e][:, :3072]) for e in range(2)])
    wfg = np.zeros((2, 1024, 128), np.float32)
    for e in range(2):
        for g in range(4):
            wfg[e][:, g * 32:g * 32 + 8] = ewin[e][:, 3072:3080]
    wfg_t = np.stack([tile_w(wfg[e])[0] for e in range(2)])
    dwin = f(inp["diff_w_in"])
    win_d = np.stack([tile_w(dwin[o]) for o in range(2)])
    wg = np.stack([tile_w(f(inp["ffn_w_gate"])[l]) for l in range(4)])
    wu = np.stack([tile_w(f(inp["ffn_w_up"])[l]) for l in range(4)])
    wd = np.stack([tile_w(f(inp["ffn_w_down"])[l]) for l in range(4)])
    inp32 = {k: f(v) for k, v in inp.items() if k != "x"}
    return dict(params=build_params(inp32), cmat=build_cmat(), win_e=win_e, wfg=wfg_t,
                wout_e=f(inp["even_w_out"]), win_d=win_d, wout_d=f(inp["diff_w_out"]), wg=wg, wu=wu, wd=wd)


_NC_CACHE = {}


def kernel(**inputs):
    x = np.asarray(inputs["x"], dtype=np.float32)
    B, S, _ = x.shape
    ncores = 8
    nseq = B // ncores
    key = (S, nseq)
    if key not in _NC_CACHE:
        _NC_CACHE[key] = build_program(S=S, NSEQ=nseq, LAYERS=4)
    nc = _NC_CACHE[key]
    shared = make_shared_inputs(inputs)
    in_maps = []
    for c in range(ncores):
        xs = x[c * nseq:(c + 1) * nseq]
        xT = np.ascontiguousarray(xs.transpose(0, 2, 1)).reshape(nseq, 8, 128, S)
        m = dict(shared)
        m["xT"] = xT
        in_maps.append(m)
    res = run_bass_kernel_spmd(nc, in_maps, core_ids=list(range(ncores)))
    out = np.empty((B, S, D), np.float32)
    for c in range(ncores):
        yT = np.asarray(res.results[c]["yT"]).reshape(nseq, D, S)
        out[c * nseq:(c + 1) * nseq] = yT.transpose(0, 2, 1)
    return out
```
